# Optimizing a Trainium2 kernel written in Bass

```python
import math
import jax, jax.numpy as jnp
from jax import lax
import numpy as np

D_MODEL = 1024
BATCH = 4
SEQ = 8192
DEPTH = 1

DN_HEADS = 8
DN_HEAD_DIM = 128
DN_QK_DIM = DN_HEADS * DN_HEAD_DIM
DN_V_DIM = DN_HEADS * DN_HEAD_DIM
CHUNK = 64
CONV_K = 4
LRU_HEADS = 10
LRU_BLOCK = 128
LRU_WIDTH = LRU_HEADS * LRU_BLOCK
LRU_C = 8.0
D_FF = -(-8 * D_MODEL // (3 * 256)) * 256
DEEPNORM_ALPHA = (2.0 * DEPTH) ** 0.25
DEEPNORM_BETA = (8.0 * DEPTH) ** -0.25
LN_EPS = 1e-5
RMS_EPS = 1e-6
IN_SIZES = [DN_QK_DIM, DN_QK_DIM, DN_V_DIM, DN_V_DIM, DN_HEADS, DN_HEADS,
            LRU_WIDTH, LRU_WIDTH, D_MODEL, D_MODEL]
IN_COLS = sum(IN_SIZES)
IN_SPLIT_IDX = [int(s) for s in np.cumsum(IN_SIZES)[:-1]]

kernel_name = "hybrid_gdn_rglru_swiglu_deepnorm"


def layer_norm(x, g, b):
    xf = x.astype(jnp.float32)
    mu = jnp.mean(xf, axis=-1, keepdims=True)
    var = jnp.mean(jnp.square(xf - mu), axis=-1, keepdims=True)
    y = (xf - mu) * lax.rsqrt(var + LN_EPS) * g.astype(jnp.float32) + b.astype(jnp.float32)
    return y.astype(x.dtype)


def causal_depthwise_conv(x, w):
    K, C = w.shape
    return lax.conv_general_dilated(
        x, w[:, None, :].astype(x.dtype), window_strides=(1,), padding=[(K - 1, 0)],
        dimension_numbers=("NWC", "WIO", "NWC"), feature_group_count=C)


def l2_normalize(x):
    return x * lax.rsqrt(jnp.sum(jnp.square(x), axis=-1, keepdims=True) + RMS_EPS)


def gated_delta_rule_chunked(q, k, v, g, beta):
    B, T, H, Dk = q.shape
    Dv = v.shape[-1]
    N = T // CHUNK

    def chunks(t):
        return t.reshape(B, N, CHUNK, H, -1).transpose(1, 0, 3, 2, 4)

    q = chunks(q) * (Dk ** -0.5)
    k = chunks(k)
    v = chunks(v)
    g = chunks(g[..., None])[..., 0]
    beta = chunks(beta[..., None])[..., 0]
    G = jnp.cumsum(g, axis=-1)

    idx = jnp.arange(CHUNK)
    causal = idx[:, None] >= idx[None, :]
    strict = idx[:, None] > idx[None, :]
    decay = jnp.exp(jnp.where(causal, G[..., :, None] - G[..., None, :], -jnp.inf))

    k_beta = k * beta[..., None]
    A = jnp.where(strict, jnp.einsum("nbhid,nbhjd->nbhij", k_beta, k) * decay, 0.0)
    T_mat = A + jnp.eye(CHUNK, dtype=A.dtype)
    rhs = jnp.concatenate([v * beta[..., None], k_beta * jnp.exp(G)[..., None]], axis=-1)
    sol = lax.linalg.triangular_solve(T_mat, rhs, left_side=True, lower=True, unit_diagonal=True)
    u = sol[..., :Dv]
    w = sol[..., Dv:]
    qk_intra = jnp.einsum("nbhid,nbhjd->nbhij", q, k) * decay

    def step(S, inp):
        q_c, k_c, u_c, w_c, G_c, qk_c = inp
        v_new = u_c - jnp.einsum("bhcd,bhde->bhce", w_c, S)
        o_c = (jnp.einsum("bhcd,bhde->bhce", q_c * jnp.exp(G_c)[..., None], S)
               + jnp.einsum("bhij,bhje->bhie", qk_c, v_new))
        G_last = G_c[..., -1]
        k_dec = k_c * jnp.exp(G_last[..., None] - G_c)[..., None]
        S = S * jnp.exp(G_last)[..., None, None] + jnp.einsum("bhcd,bhce->bhde", k_dec, v_new)
        return S, o_c

    S0 = jnp.zeros((B, H, Dk, Dv), jnp.float32)
    _, o = lax.scan(step, S0, (q, k, u, w, G, qk_intra))
    return o.transpose(1, 0, 3, 2, 4).reshape(B, T, H, Dv)


def rg_lru(u, w_a, b_a, w_i, b_i, lam):
    B, T, W = u.shape
    uf = u.astype(jnp.float32)
    ub = uf.reshape(B, T, LRU_HEADS, LRU_BLOCK)
    r = jax.nn.sigmoid(jnp.einsum("bthi,hij->bthj", ub, w_a.astype(jnp.float32)).reshape(B, T, W) + b_a)
    i = jax.nn.sigmoid(jnp.einsum("bthi,hij->bthj", ub, w_i.astype(jnp.float32)).reshape(B, T, W) + b_i)
    log_a = -LRU_C * r * jax.nn.softplus(-lam.astype(jnp.float32))
    a = jnp.exp(log_a)
    mult = jnp.sqrt(-jnp.expm1(2.0 * log_a))
    mult = jnp.where(jnp.arange(T)[None, :, None] == 0, 1.0, mult)
    b_in = mult * (i * uf)

    def combine(e1, e2):
        return (e1[0] * e2[0], e2[0] * e1[1] + e2[1])

    _, h = lax.associative_scan(combine, (a, b_in), axis=1)
    return h


def hybrid_mixer(x, w_in, dn_conv_w, dn_A_log, dn_dt_bias, dn_norm_w,
                 lru_conv_w, lru_conv_b, lru_w_a, lru_b_a, lru_w_i, lru_b_i, lru_lambda,
                 w_branch_dn, w_branch_lru, b_merge_gate, w_out):
    B, T, _ = x.shape
    proj = x @ w_in
    q, k, v, z, b_raw, a_raw, lru_x, lru_g, gate_dn, gate_lru = jnp.split(proj, IN_SPLIT_IDX, axis=-1)

    qkv = jax.nn.silu(causal_depthwise_conv(jnp.concatenate([q, k, v], axis=-1), dn_conv_w))
    q, k, v = jnp.split(qkv.astype(jnp.float32), [DN_QK_DIM, 2 * DN_QK_DIM], axis=-1)
    q = l2_normalize(q.reshape(B, T, DN_HEADS, DN_HEAD_DIM))
    k = l2_normalize(k.reshape(B, T, DN_HEADS, DN_HEAD_DIM))
    v = v.reshape(B, T, DN_HEADS, DN_HEAD_DIM)
    beta = jax.nn.sigmoid(b_raw.astype(jnp.float32))
    g = -jnp.exp(dn_A_log.astype(jnp.float32)) * jax.nn.softplus(
        a_raw.astype(jnp.float32) + dn_dt_bias.astype(jnp.float32))
    o = gated_delta_rule_chunked(q, k, v, g, beta)
    o = o * lax.rsqrt(jnp.mean(jnp.square(o), axis=-1, keepdims=True) + RMS_EPS) * dn_norm_w.astype(jnp.float32)
    o = o * jax.nn.silu(z.astype(jnp.float32).reshape(B, T, DN_HEADS, DN_HEAD_DIM))
    y_dn = o.reshape(B, T, DN_V_DIM).astype(x.dtype) @ w_branch_dn

    u = causal_depthwise_conv(lru_x, lru_conv_w) + lru_conv_b
    h = rg_lru(u, lru_w_a, lru_b_a, lru_w_i, lru_b_i, lru_lambda)
    y_lru = (h * jax.nn.gelu(lru_g.astype(jnp.float32))).astype(x.dtype) @ w_branch_lru

    gates = jax.nn.sigmoid(jnp.concatenate([gate_dn, gate_lru], axis=-1) + b_merge_gate)
    g_dn, g_lru = jnp.split(gates, 2, axis=-1)
    merged = g_dn * y_dn + g_lru * y_lru
    return merged @ w_out


def swiglu(x, w_gate, w_up, w_down):
    return (jax.nn.silu(x @ w_gate) * (x @ w_up)) @ w_down


def setup_inputs(seed: int = 0) -> dict:
    key = jax.random.key(seed)
    keys = jax.random.split(key, 32)
    L = DEPTH

    def nrm(k, shape, scale):
        return jax.random.normal(k, shape, jnp.float32) * scale

    x = nrm(keys[0], (BATCH, SEQ, D_MODEL), 1.0)
    col_scale = jnp.concatenate([
        jnp.full((s,), DEEPNORM_BETA if i == 2 else 1.0, jnp.float32)
        for i, s in enumerate(IN_SIZES)])
    w_in = nrm(keys[1], (L, D_MODEL, IN_COLS), D_MODEL ** -0.5) * col_scale
    dn_conv_w = nrm(keys[2], (L, CONV_K, 2 * DN_QK_DIM + DN_V_DIM), CONV_K ** -0.5)
    dn_A_log = jnp.log(jax.random.uniform(keys[3], (L, DN_HEADS), jnp.float32, 1.0, 16.0))
    dt = jnp.exp(jax.random.uniform(keys[4], (L, DN_HEADS), jnp.float32, math.log(1e-3), math.log(1e-1)))
    dn_dt_bias = dt + jnp.log(-jnp.expm1(-dt))
    dn_norm_w = 1.0 + nrm(keys[5], (L, DN_HEAD_DIM), 0.02)
    lru_conv_w = nrm(keys[6], (L, CONV_K, LRU_WIDTH), CONV_K ** -0.5)
    lru_conv_b = nrm(keys[7], (L, LRU_WIDTH), 0.02)
    lru_w_a = nrm(keys[8], (L, LRU_HEADS, LRU_BLOCK, LRU_BLOCK), LRU_BLOCK ** -0.5)
    lru_b_a = nrm(keys[9], (L, LRU_WIDTH), 0.02)
    lru_w_i = nrm(keys[10], (L, LRU_HEADS, LRU_BLOCK, LRU_BLOCK), LRU_BLOCK ** -0.5)
    lru_b_i = nrm(keys[11], (L, LRU_WIDTH), 0.02)
    a0 = jax.random.uniform(keys[12], (L, LRU_WIDTH), jnp.float32, 0.9, 0.999) ** (1.0 / LRU_C)
    lru_lambda = jnp.log(a0) - jnp.log1p(-a0)
    w_branch_dn = nrm(keys[13], (L, DN_V_DIM, D_MODEL), DN_V_DIM ** -0.5 * DEEPNORM_BETA)
    w_branch_lru = nrm(keys[14], (L, LRU_WIDTH, D_MODEL), LRU_WIDTH ** -0.5 * DEEPNORM_BETA)
    b_merge_gate = nrm(keys[15], (L, 2 * D_MODEL), 0.02)
    w_out = nrm(keys[16], (L, D_MODEL, D_MODEL), D_MODEL ** -0.5 * DEEPNORM_BETA)
    ln1_g = 1.0 + nrm(keys[17], (L, D_MODEL), 0.02)
    ln1_b = nrm(keys[18], (L, D_MODEL), 0.02)
    w_ffn_gate = nrm(keys[19], (L, D_MODEL, D_FF), D_MODEL ** -0.5 * DEEPNORM_BETA)
    w_ffn_up = nrm(keys[20], (L, D_MODEL, D_FF), D_MODEL ** -0.5 * DEEPNORM_BETA)
    w_ffn_down = nrm(keys[21], (L, D_FF, D_MODEL), D_FF ** -0.5 * DEEPNORM_BETA)
    ln2_g = 1.0 + nrm(keys[22], (L, D_MODEL), 0.02)
    ln2_b = nrm(keys[23], (L, D_MODEL), 0.02)
    return {"x": x, "w_in": w_in, "dn_conv_w": dn_conv_w, "dn_A_log": dn_A_log,
            "dn_dt_bias": dn_dt_bias, "dn_norm_w": dn_norm_w, "lru_conv_w": lru_conv_w,
            "lru_conv_b": lru_conv_b, "lru_w_a": lru_w_a, "lru_b_a": lru_b_a,
            "lru_w_i": lru_w_i, "lru_b_i": lru_b_i, "lru_lambda": lru_lambda,
            "w_branch_dn": w_branch_dn, "w_branch_lru": w_branch_lru,
            "b_merge_gate": b_merge_gate, "w_out": w_out, "ln1_g": ln1_g, "ln1_b": ln1_b,
            "w_ffn_gate": w_ffn_gate, "w_ffn_up": w_ffn_up, "w_ffn_down": w_ffn_down,
            "ln2_g": ln2_g, "ln2_b": ln2_b}


def reference(x, w_in, dn_conv_w, dn_A_log, dn_dt_bias, dn_norm_w, lru_conv_w, lru_conv_b,
              lru_w_a, lru_b_a, lru_w_i, lru_b_i, lru_lambda, w_branch_dn, w_branch_lru,
              b_merge_gate, w_out, ln1_g, ln1_b, w_ffn_gate, w_ffn_up, w_ffn_down,
              ln2_g, ln2_b):
    h = x
    for l in range(DEPTH):
        mix = hybrid_mixer(h, w_in[l], dn_conv_w[l], dn_A_log[l], dn_dt_bias[l], dn_norm_w[l],
                           lru_conv_w[l], lru_conv_b[l], lru_w_a[l], lru_b_a[l], lru_w_i[l],
                           lru_b_i[l], lru_lambda[l], w_branch_dn[l], w_branch_lru[l],
                           b_merge_gate[l], w_out[l])
        h = layer_norm(DEEPNORM_ALPHA * h + mix, ln1_g[l], ln1_b[l])
        ff = swiglu(h, w_ffn_gate[l], w_ffn_up[l], w_ffn_down[l])
        h = layer_norm(DEEPNORM_ALPHA * h + ff, ln2_g[l], ln2_b[l])
    return h
```

```python
import numpy as np
import concourse.bass as bass
import concourse.mybir as mybir
from concourse.bass_utils import run_bass_kernel_spmd
from contextlib import ExitStack

F32 = mybir.dt.float32
BF16 = mybir.dt.bfloat16
AF = mybir.ActivationFunctionType
ALU = mybir.AluOpType

F32R = mybir.dt.float32r
_DT_SIZE = {mybir.dt.float32: 4, mybir.dt.bfloat16: 2, mybir.dt.float32r: 4}

D = 1024
H = 8
HL = 10
DFF = 2816
NFC = 22
TB = 512
ALPHA = 2.0 ** 0.25
LN_EPS = 1e-5
RMS_EPS = 1e-6
BIG = 1.0e5
GC0 = 0.7978845608028654
GC1 = 0.044715
C_Q, C_K, C_V, C_Z, C_BD, C_LX, C_LG, C_GD, C_GL = 0, 1024, 2048, 3072, 4096, 4112, 5392, 6672, 7696

CS_ID, CS_U, CS_ONE, CS_POSL, CS_POSU, CS_E, CS_EP = 0, 128, 256, 384, 512, 640, 896
NCST = 896 + 2048
CH_CWQ, CH_CWL, CH_CBL, CH_BA, CH_BI, CH_LAM, CH_BM, CH_NW = 0, 96, 136, 146, 156, 166, 176, 192
NCHP = 193


def _box(ap):
    t = ap.tensor
    name = t.name
    dims = [list(d) for d in ap.ap]
    esz = _DT_SIZE[ap.dtype]
    if "dram" in str(type(t)).lower():
        lo = ap.offset
        hi = lo
        for st, cnt in dims:
            if cnt > 1:
                hi += abs(st) * (cnt - 1)
        return (name, 0, 1, lo * esz, (hi + 1) * esz)
    pstride = dims[0][0]
    pcnt = dims[0][1]
    if pstride == 0:
        p0, f0 = 0, ap.offset
    else:
        p0 = ap.offset // pstride
        f0 = ap.offset - p0 * pstride
    ext = 0
    for st, cnt in dims[1:]:
        if cnt > 1:
            ext += abs(st) * (cnt - 1)
    if name.startswith("ps") and name[2:].isdigit():
        return (name, 0, 128, 0, 2048)
    return (name, p0, p0 + pcnt, f0 * esz, (f0 + ext + 1) * esz)


def _ovl(a, b):
    return a[1] < b[2] and b[1] < a[2] and a[3] < b[4] and b[3] < a[4]


def _contains(a, b):
    return a[1] <= b[1] and a[2] >= b[2] and a[3] <= b[3] and a[4] >= b[4]


class Sched:
    ENGS = ("tensor", "vector", "scalar", "gpsimd", "sync")

    def __init__(self, nc, n_dma_sems):
        self.nc = nc
        self.q = {e: [] for e in self.ENGS}
        self.cnt = {e: 0 for e in self.ENGS}
        self.sem = {}
        self.waited = {e: {} for e in self.ENGS}
        self.n_dma = n_dma_sems
        self.dma_sems = []
        self.dma_cnt = [0] * n_dma_sems
        self.dma_rr = 0
        self.dma_rr_e = {}
        self.recs = {}
        self.semobjs = {}
        self.n_wait = 0
        self.n_ins = 0
        self.efree = {e: 0.0 for e in self.ENGS}
        self.tfin = {}
        self.last_fin = 0.0

    def _est(self, eng, deps_tokens, dur, hop=0.0):
        t = self.efree[eng]
        for tk in deps_tokens:
            t = max(t, self.tfin.get(tk, 0.0) + (hop if tk[0] != ("E", eng) else 0.0))
        t += dur
        self.efree[eng] = t
        self.last_fin = max(self.last_fin, t) if self._track_stream else self.last_fin
        return t

    _track_stream = False

    def set_sems(self, eng_sems, dma_sems):
        for e, s in eng_sems.items():
            self.sem[e] = s
            self.semobjs[("E", e)] = s
        self.dma_sems = dma_sems
        for i, s in enumerate(dma_sems):
            self.semobjs[("D", i)] = s

    def _need(self, eng, tok, deps):
        key, val = tok
        if self.waited[eng].get(key, 0) >= val:
            return
        if key == ("E", eng) and eng == "tensor":
            return
        if val > deps.get(key, 0):
            deps[key] = val

    def _collect(self, eng, reads, writes):
        deps = {}
        rb = [_box(a) for a in reads]
        wb = [_box(a) for a in writes]
        for b in rb:
            rec = self.recs.get(b[0])
            if rec is None:
                continue
            for (ob, tok) in rec["w"]:
                if _ovl(b, ob):
                    self._need(eng, tok, deps)
            if b[0].startswith("ps") and b[0][2:].isdigit():
                for (ob, tok) in rec["r"]:
                    if tok[0] != ("E", eng):
                        self._need(eng, tok, deps)
        for b in wb:
            rec = self.recs.get(b[0])
            if rec is None:
                continue
            for (ob, tok) in rec["w"]:
                if _ovl(b, ob):
                    self._need(eng, tok, deps)
            for (ob, tok) in rec["r"]:
                if _ovl(b, ob):
                    self._need(eng, tok, deps)
        return deps, rb, wb

    def _record(self, tok, rb, wb):
        for b in wb:
            rec = self.recs.setdefault(b[0], {"w": [], "r": []})
            rec["w"] = [(ob, t) for (ob, t) in rec["w"] if not _contains(b, ob)]
            rec["r"] = [(ob, t) for (ob, t) in rec["r"] if not _contains(b, ob)]
            rec["w"].append((b, tok))
        for b in rb:
            rec = self.recs.setdefault(b[0], {"w": [], "r": []})
            rec["r"] = [(ob, t) for (ob, t) in rec["r"]
                        if not (t[0] == tok[0] and _contains(b, ob))]
            rec["r"].append((b, tok))

    def _emit_waits(self, eng, deps):
        for key, val in deps.items():
            self.waited[eng][key] = val
            s = self.semobjs[key]
            self.q[eng].append(lambda e, s=s, val=val: e.wait_ge(s, val))
            self.n_wait += 1

    def _dep_tokens(self, rb, wb):
        toks = []
        for b in rb:
            rec = self.recs.get(b[0])
            if rec:
                toks += [t for (ob, t) in rec["w"] if _ovl(b, ob)]
        for b in wb:
            rec = self.recs.get(b[0])
            if rec:
                toks += [t for (ob, t) in rec["w"] if _ovl(b, ob)]
                toks += [t for (ob, t) in rec["r"] if _ovl(b, ob)]
        return toks

    def op(self, eng, fn, reads=(), writes=(), dur=None):
        deps, rb, wb = self._collect(eng, reads, writes)
        dtoks = self._dep_tokens(rb, wb)
        self._emit_waits(eng, deps)
        self.cnt[eng] += 1
        idx = self.cnt[eng]
        s = self.sem[eng]
        self.q[eng].append(lambda e, fn=fn, s=s: fn(e).then_inc(s, 1))
        tok = (("E", eng), idx)
        if dur is None:
            n = 1
            if writes:
                for st_, cnt_ in list(writes[0].ap)[1:]:
                    n *= cnt_
            if eng == "tensor":
                dt_in = reads[0].dtype if reads else BF16
                passes = 4 if dt_in == F32 else (2 if dt_in == F32R else 1)
                dur = max(0.06, n / 1200.0 * passes) + 0.02
            elif eng == "scalar":
                dur = 0.22 + n / 1200.0
            elif eng == "gpsimd":
                dur = 0.25 + n / 520.0
            else:
                dur = 0.10 + n / 900.0
        self.tfin[tok] = self._est(eng, dtoks, dur, hop=0.2)
        self._record(tok, rb, wb)
        self.n_ins += 1
        return tok

    def dma(self, out, in_, eng="sync"):
        half = self.n_dma // 2
        base = 0 if eng == "sync" else half
        rr = self.dma_rr_e.get(eng, 0)
        self.dma_rr_e[eng] = (rr + 1) % half
        k = base + rr
        deps, rb, wb = self._collect(eng, [in_], [out])
        prev = self.dma_cnt[k] * 16
        if prev > 0 and self.waited[eng].get(("D", k), 0) < prev:
            deps[("D", k)] = max(deps.get(("D", k), 0), prev)
        dtoks = self._dep_tokens(rb, wb)
        self._emit_waits(eng, deps)
        self.dma_cnt[k] += 1
        val = self.dma_cnt[k] * 16
        s = self.dma_sems[k]
        self.q[eng].append(lambda e, s=s, out=out, in_=in_: e.dma_start(out=out, in_=in_).then_inc(s, 16))
        tok = (("D", k), val)
        t0 = self.efree[eng]
        for tk in dtoks:
            t0 = max(t0, self.tfin.get(tk, 0.0))
        self.efree[eng] = t0 + 0.65
        self.tfin[tok] = t0 + 3.0
        self._record(tok, rb, wb)
        self.n_ins += 1
        return tok

    def mm(self, out, lhsT, rhs, start=True, stop=True):
        return self.op("tensor", lambda e: e.matmul(out, lhsT, rhs, start=start, stop=stop),
                       reads=[lhsT, rhs], writes=[out])

    def transpose(self, out, in_, ident):
        return self.op("tensor", lambda e: e.transpose(out, in_, ident), reads=[in_, ident], writes=[out])

    def act(self, out, in_, func, bias=None, scale=1.0):
        reads = [in_]
        kw = {}
        if bias is not None:
            kw["bias"] = bias
            if not isinstance(bias, (int, float)):
                reads.append(bias)
        if not isinstance(scale, (int, float)):
            reads.append(scale)
        return self.op("scalar", lambda e: e.activation(out=out, in_=in_, func=func, scale=scale, **kw),
                       reads=reads, writes=[out])

    def copy(self, eng, out, in_):
        if eng == "scalar":
            return self.act(out, in_, AF.Copy)
        return self.op(eng, lambda e: e.tensor_copy(out=out, in_=in_), reads=[in_], writes=[out])

    def tt(self, out, in0, in1, op, eng="vector"):
        return self.op(eng, lambda e: e.tensor_tensor(out=out, in0=in0, in1=in1, op=op),
                       reads=[in0, in1], writes=[out])

    def ts(self, out, in0, s1, op0, s2=None, op1=None, eng="vector"):
        reads = [in0]
        if not isinstance(s1, (int, float)):
            reads.append(s1)
        if s2 is not None and not isinstance(s2, (int, float)):
            reads.append(s2)
        if op1 is None:
            return self.op(eng, lambda e: e.tensor_scalar(out=out, in0=in0, scalar1=s1, scalar2=None, op0=op0),
                           reads=reads, writes=[out])
        return self.op(eng, lambda e: e.tensor_scalar(out=out, in0=in0, scalar1=s1, scalar2=s2,
                                                      op0=op0, op1=op1), reads=reads, writes=[out])

    def stt(self, out, in0, scalar, in1, op0, op1):
        reads = [in0, in1]
        if not isinstance(scalar, (int, float)):
            reads.append(scalar)
        return self.op("vector", lambda e: e.scalar_tensor_tensor(out=out, in0=in0, scalar=scalar, in1=in1,
                                                                  op0=op0, op1=op1),
                       reads=reads, writes=[out])

    def memset(self, eng, ap, val):
        return self.op(eng, lambda e: e.memset(ap, val), reads=[], writes=[ap])

    def replay(self, block):
        finals = []
        for e in self.ENGS:
            if self.cnt[e] > 0:
                finals.append((self.sem[e], self.cnt[e]))
        for k in range(self.n_dma):
            if self.dma_cnt[k] > 0:
                finals.append((self.dma_sems[k], self.dma_cnt[k] * 16))
        qs = self.q

        def run(engname):
            def body(e):
                for f in qs[engname]:
                    f(e)
                if engname == "sync":
                    for s, v in finals:
                        e.wait_ge(s, v)
            return body
        block.sync(run("sync"))
        block.tensor(run("tensor"))
        block.vector(run("vector"))
        block.scalar(run("scalar"))
        block.gpsimd(run("gpsimd"))


class _Stop(Exception):
    pass


def build(NPRE, NMAIN, debug=False, stage=99):
    nc = bass.Bass("TRN2", target_bir_lowering=False)
    dbg_names = []
    TPRE, TMAIN = TB * NPRE, TB * NMAIN

    def dram(name, shape, kind="ExternalInput"):
        return nc.dram_tensor(name, shape, F32, kind=kind).ap()
    xT_pre = dram("xT_pre", [D, TPRE])
    xT_main = dram("xT_main", [D, TMAIN])
    x_main = dram("x_main", [TMAIN, D])
    w_in = dram("w_in", [D, 8720])
    lru_wa = dram("lru_wa", [HL, 128, 128])
    lru_wi = dram("lru_wi", [HL, 128, 128])
    w_bdn = dram("w_bdn", [1024, D])
    w_blru = dram("w_blru", [1280, D])
    w_out = dram("w_out", [D, D])
    w_g = dram("w_g", [D, DFF])
    w_u = dram("w_u", [D, DFF])
    w_d = dram("w_d", [DFF, D])
    cst_d = dram("cst_in", [128, NCST])
    chp_d = dram("chp_in", [128, NCHP])
    rowp_d = dram("rowp_in", [128, 16])
    lnp_d = dram("lnp_in", [128, 4 * D])
    flag_d = dram("flag_in", [128, 1])
    out_d = dram("out", [TMAIN, D], kind="ExternalOutput")

    with ExitStack() as es:
        def sb(name, shape, dt):
            return es.enter_context(nc.sbuf_tensor(name, shape, dt))
        S = Sched(nc, n_dma_sems=20)
        eng_sems = {e: es.enter_context(nc.semaphore("sem_" + e)) for e in Sched.ENGS}
        dma_sems = [es.enter_context(nc.semaphore("dsem%d" % i)) for i in range(20)]
        S.set_sems(eng_sems, dma_sems)

        cst = sb("cst", [128, NCST], F32)
        chp = sb("chp", [128, NCHP], F32)
        rowp = sb("rowp", [128, 16], F32)
        flag = sb("flag", [128, 2], F32)
        der = sb("der", [128, 64], F32)
        cbf = sb("cbf", [128, 128 + 128 + 256], BF16)
        identr = sb("identr", [128, 128], F32R)
        wbd = sb("wbd", [128, 8, 16], BF16)
        xT = [sb("xT%d" % i, [128, 8, TB], BF16) for i in range(2)]
        wslot = [sb("wslot%d" % i, [128, 5120], BF16) for i in range(5)]
        qT = sb("qT", [128, 8, TB], BF16)
        kT = sb("kT", [128, 8, TB], BF16)
        vT = sb("vT", [128, 8, TB], BF16)
        ogT = sb("ogT", [128, 8, TB], BF16)
        lroT = sb("lroT", [128, HL, TB], BF16)
        S32 = sb("S32", [128, 8, 128], F32)
        Sbf = sb("Sbf", [128, 8, 128], BF16)
        hst = sb("hst", [128, HL], F32)
        tailq = sb("tailq", [128, 24, 4], BF16)
        taill = sb("taill", [128, HL, 4], BF16)
        SCR_BYTES = 66 * 1024
        scr = sb("scr", [128, SCR_BYTES // 4], F32)
        ps = [es.enter_context(nc.psum_tensor("ps%d" % i, [128, 512], F32)) for i in range(8)]
        pstate = {"i": 0}

        pools = {"all": list(range(8)), "D0": [0, 1, 2], "D1": [3, 4, 5], "L": [6, 7]}
        pidx = {"all": 0, "D0": 0, "D1": 0, "L": 0}

        def nb(pool="all"):
            lst = pools[pool]
            i = pidx[pool]
            pidx[pool] = (i + 1) % len(lst)
            return ps[lst[i]]

        evs = {"i": 0}

        def ev():
            evs["i"] ^= 1
            return "scalar" if evs["i"] else "vector"

        def carve_factory():
            st = {"off": 0}

            def carve(shape, dt):
                n = 1
                for s_ in shape:
                    n *= s_
                nbytes = n * _DT_SIZE[dt]
                nbytes = (nbytes + 3) // 4 * 4
                off = st["off"]
                st["off"] += nbytes
                assert st["off"] <= SCR_BYTES, ("scratch overflow", st["off"])
                v = scr[:, off // 4:(off + nbytes) // 4]
                if dt == BF16:
                    v = v.bitcast(BF16)
                    v = v[:, 0:n]
                elif dt == F32R:
                    v = v.bitcast(F32R)
                if len(shape) == 2:
                    return v.rearrange("p (a b) -> p a b", a=shape[0])
                if len(shape) == 3:
                    return v.rearrange("p (a b c) -> p a b c", a=shape[0], b=shape[1])
                return v
            return carve

        c1 = carve_factory()
        xin = [c1([TB + 4], BF16) for _ in range(3)]
        dgw = [c1([4, 128], BF16) for _ in range(3)]
        cv = [None, None, c1([TB], F32)]
        tq = [c1([TB], F32) for _ in range(2)]
        betah = c1([32], F32)
        betan = c1([32], F32)
        t0buf = c1([2], F32)
        tmpA = [c1([TB], F32) for _ in range(2)]
        tmpB = [c1([TB], F32) for _ in range(2)]
        tmpC = c1([TB], F32)
        tmpD = c1([TB], F32)
        tmpE = c1([TB], F32)
        ubf = c1([TB], BF16)
        sqb = [c1([TB], BF16) for _ in range(2)]
        rs16b = [c1([TB], F32) for _ in range(2)]
        rs16 = rs16b[0]
        bdv = c1([4, 16], F32)
        beta = c1([4, 8], F32)
        gg = c1([4, 8], F32)
        tsm = [c1([4, 8], F32) for _ in range(4)]
        Gs = c1([32], F32)
        eG = c1([32], F32)
        eGt = c1([32], F32)
        eGr = c1([32], F32)
        bG = c1([32], F32)
        XpT1 = [c1([4, 128], F32) for _ in range(2)]
        T2b = [c1([4, 128], F32) for _ in range(2)]
        GL = [c1([4, 128], BF16) for _ in range(2)]
        GU = [c1([4, 128], BF16) for _ in range(2)]
        eGw = [c1([4, 128], BF16) for _ in range(2)]
        xrt = sb("xrt", [128, 6, 4, 128], F32R)
        Xa = [[xrt[:, 0], xrt[:, 1]]]
        XTa = [[xrt[:, 2], xrt[:, 3]]]
        RTa = [[xrt[:, 4], xrt[:, 5]]]
        Xa = Xa * 2; XTa = XTa * 2; RTa = RTa * 2
        RTb = [c1([4, 128], BF16) for _ in range(2)]
        PT = [c1([4, 128], BF16) for _ in range(2)]
        nwT = [c1([4, 128], BF16) for _ in range(2)]
        vnew = [c1([4, 128], BF16) for _ in range(2)]
        qe = [c1([4, 128], BF16) for _ in range(2)]
        kbe = [c1([4, 128], BF16) for _ in range(2)]
        kdc = [c1([4, 128], BF16) for _ in range(2)]
        vbt = [c1([4, 128], BF16) for _ in range(2)]
        c2 = carve_factory()
        h1 = c2([4, D], F32)
        h1T = qT[:]
        actT = c2([NFC, TB], BF16)
        mrgT = kT[:]
        lnp = c2([2, D], F32)
        h1b = c2([D], BF16)
        gt1 = c2([TB], F32)
        gt2 = c2([TB], F32)
        xtok = c2([D], F32)
        m1 = xtok[:, 0:TB]
        m2 = xtok[:, TB:2 * TB]
        stats = c2([4, 2, 6], F32)
        mv = c2([4, 2], F32)
        rstd = c2([4], F32)

        ident = cst[:, CS_ID:CS_ID + 128]
        Umat = cst[:, CS_U:CS_U + 128]
        ones = cst[:, CS_ONE:CS_ONE + 128]
        POSL = cst[:, CS_POSL:CS_POSL + 128]
        POSU = cst[:, CS_POSU:CS_POSU + 128]
        Esel = cst[:, CS_E:CS_E + 256]
        Ep = cst[0:16, CS_EP:CS_EP + 2048]
        ident_bf = cbf[:, 0:128]
        ones_bf = cbf[:, 128:256]
        E_bf = cbf[:, 256:512]
        DR_HBA, DR_HBI, DR_HLS, DR_HBM, DR_OMF, DR_NWH = 0, 10, 20, 30, 46, 47
        negA = rowp[:, 0:8]
        dtb = rowp[:, 8:16]

        S.dma(cst[:], cst_d)
        S.dma(chp[:], chp_d)
        S.dma(rowp[:], rowp_d)
        S.dma(flag[:, 0:1], flag_d)
        S.copy("vector", cbf[:, 0:128], ident)
        S.copy("vector", identr[:], ident)
        S.copy("vector", cbf[:, 128:256], ones)
        S.copy("vector", cbf[:, 256:512], Esel)
        S.act(negA, negA, AF.Exp)
        S.ts(negA, negA, -1.0, ALU.mult)
        S.ts(der[:, DR_HBA:DR_HBA + 10], chp[:, CH_BA:CH_BA + 10], 0.5, ALU.mult)
        S.ts(der[:, DR_HBI:DR_HBI + 10], chp[:, CH_BI:CH_BI + 10], 0.5, ALU.mult)
        S.ts(der[:, DR_HBM:DR_HBM + 16], chp[:, CH_BM:CH_BM + 16], 0.5, ALU.mult)
        S.act(der[:, DR_HLS:DR_HLS + 10], chp[:, CH_LAM:CH_LAM + 10], AF.Exp, scale=-1.0)
        S.act(der[:, DR_HLS:DR_HLS + 10], der[:, DR_HLS:DR_HLS + 10], AF.Ln, bias=1.0)
        S.ts(der[:, DR_HLS:DR_HLS + 10], der[:, DR_HLS:DR_HLS + 10], -4.0, ALU.mult)
        S.ts(der[:, DR_OMF:DR_OMF + 1], flag[:, 0:1], -1.0, ALU.mult, 1.0, ALU.add)
        S.ts(der[:, DR_NWH:DR_NWH + 1], chp[:, CH_NW:CH_NW + 1], 0.5, ALU.mult)
        S.memset("vector", S32[:], 0.0)
        S.memset("vector", Sbf[:], 0.0)
        S.memset("vector", hst[:], 0.0)
        S.memset("vector", tailq[:], 0.0)
        S.memset("vector", taill[:], 0.0)

        w_in_v = w_in.rearrange("(kc p) c -> p kc c", p=128)

        def g_win(name, c0, n):
            return (name, w_in_v[:, :, c0:c0 + n], 8, n)

        def block_groups(full, need_q=False):
            g = []
            if full:
                bdn_v = w_bdn.rearrange("(kc p) c -> p kc c", p=128)
                blru_v = w_blru.rearrange("(kc p) c -> p kc c", p=128)
                wo_v = w_out.rearrange("(kc p) c -> p kc c", p=128)
                wg_v = w_g.rearrange("(kc p) c -> p kc c", p=128)
                wu_v = w_u.rearrange("(kc p) c -> p kc c", p=128)
                wd_v = w_d.rearrange("(fc p) c -> p fc c", p=128)
                for hf in range(2):
                    g += [g_win("gd%d" % hf, C_GD + 512 * hf, 512), g_win("gl%d" % hf, C_GL + 512 * hf, 512),
                          ("bdn%d" % hf, bdn_v[:, :, 512 * hf:512 * hf + 512], 8, 512),
                          ("blru%d" % hf, blru_v[:, :, 512 * hf:512 * hf + 512], 10, 512)]
                g += [("wo0", wo_v[:, :, 0:512], 8, 512), ("wo1", wo_v[:, :, 512:1024], 8, 512)]
                for i in range(6):
                    n = 512 if i < 5 else 256
                    g += [("wg%d" % i, wg_v[:, :, 512 * i:512 * i + n], 8, n),
                          ("wu%d" % i, wu_v[:, :, 512 * i:512 * i + n], 8, n)]
                for hf in range(2):
                    for j, (f0, nf) in enumerate(((0, 8), (8, 8), (16, 6))):
                        g += [("wd%d_%d" % (hf, j), wd_v[:, f0:f0 + nf, 512 * hf:512 * hf + 512], nf, 512)]
            return g

        allg = []
        for b in range(NPRE):
            allg += block_groups(False, need_q=(b == NPRE - 1))
        for b in range(NMAIN):
            allg += block_groups(True)
        ws = {"issued": 0, "used": 0, "limit": 0}
        PF = 3
        n_p2 = len(block_groups(True))
        for kc in range(8):
            S.dma(wbd[:, kc, :], w_in_v[:, kc, C_BD:C_BD + 16], eng="gpsimd")

        def wload(slot_i, src, a, bb):
            dst = wslot[slot_i][:, 0:a * bb].rearrange("p (a b) -> p a b", a=a)
            for j in range(a):
                S.dma(dst[:, j, :], src[:, j, :], eng="gpsimd")
            return dst

        def w_issue_upto(n):
            while ws["issued"] < min(n, len(allg), ws["limit"]):
                i = ws["issued"]
                name, src, a, bb = allg[i]
                slot = wslot[i % 5]
                dst = slot[:, 0:a * bb].rearrange("p (a b) -> p a b", a=a)
                for j in range(a):
                    S.dma(dst[:, j, :], src[:, j, :], eng="gpsimd")
                ws["issued"] += 1

        def wnext(name, keep=0):
            i = ws["used"]
            assert allg[i][0] == name, (allg[i][0], name)
            assert ws["issued"] <= (i - keep) + 5
            w_issue_upto(min(i + 1 + PF, (i - keep) + 5))
            assert ws["issued"] > i
            ws["used"] += 1
            _, _, a, bb = allg[i]
            return wslot[i % 5][:, 0:a * bb].rearrange("p (a b) -> p a b", a=a)

        blocks = [("pre", i) for i in range(NPRE)] + [("main", i) for i in range(NMAIN)]

        def x_load(bi):
            kind, i = blocks[bi]
            src = (xT_pre if kind == "pre" else xT_main).rearrange("(kc p) t -> p kc t", p=128)
            for kc in range(8):
                S.dma(xT[bi % 2][:, kc, :], src[:, kc, i * TB:(i + 1) * TB], eng="gpsimd")

        x_load(0)

        def dbgdump(name, ap, n=TB):
            if not debug:
                return
            t = nc.dram_tensor("dbg_" + name, [128, n], F32, kind="ExternalOutput").ap()
            dbg_names.append("dbg_" + name)
            S.dma(t, ap)

        def inproj(wv, ci, xTb, n=TB, t0=0, pool="all"):
            pb = nb(pool)
            for kc in range(8):
                S.mm(pb[:, 0:n], wv[:, kc, ci * 128:(ci + 1) * 128], xTb[:, kc, t0:t0 + n],
                     start=(kc == 0), stop=(kc == 7))
            return pb

        def conv(pb, tail, cw, idx, pool):
            xi = xin[idx]
            dg = dgw[idx]
            for k in range(4):
                S.ts(dg[:, k, :], ident_bf, cw[:, k:k + 1], ALU.mult)
            S.copy("gpsimd", xi[:, 0:3], tail[:, 0:3])
            S.copy("scalar", xi[:, 3:TB + 3], pb[:, :])
            S.copy("gpsimd", tail[:, 0:3], xi[:, TB:TB + 3])
            pc = nb(pool)
            for k in range(4):
                S.mm(pc[:, :], dg[:, k, :], xi[:, k:k + TB], start=(k == 0), stop=(k == 3))
            return pc

        def chk(k):
            if stage == k:
                raise _Stop()

        for bi, (kind, blk) in enumerate(blocks):
          try:
                full = kind == "main"
                xTb = xT[bi % 2]
                if bi + 1 < len(blocks):
                    x_load(bi + 1)
                first_main = full and blk == 0

                wv = wbd
                pbd = nb()
                for sub in range(4):
                    for kc in range(8):
                        S.mm(pbd[:, sub * 16:(sub + 1) * 16], xTb[:, kc, sub * 128:(sub + 1) * 128], wv[:, kc, :],
                             start=(kc == 0), stop=(kc == 7))
                pbd3 = pbd[:, 0:64].rearrange("p (s c) -> p s c", s=4)
                S.act(beta, pbd3[:, :, 0:8], AF.Tanh, scale=0.5)
                S.ts(beta, beta, 0.5, ALU.mult, 0.5, ALU.add)
                xa, ab, ex, ln_ = tsm
                S.tt(xa, pbd3[:, :, 8:16], dtb.unsqueeze(1).to_broadcast([128, 4, 8]), ALU.add)
                S.ts(ab, xa, -1.0, ALU.mult)
                S.tt(ab, ab, xa, ALU.max)
                S.act(ex, ab, AF.Exp, scale=-1.0)
                S.act(ln_, ex, AF.Ln, bias=1.0)
                S.ts(xa, xa, 0.0, ALU.max)
                S.tt(xa, xa, ln_, ALU.add)
                S.tt(gg, xa, negA.unsqueeze(1).to_broadcast([128, 4, 8]), ALU.mult)
                gg2 = gg.rearrange("p s h -> p (s h)")
                pG = nb()
                S.mm(pG[:, 0:32], Umat, gg2)
                S.mm(pG[:, 32:64], ones, gg2)
                S.copy("vector", Gs, pG[:, 0:32])
                S.act(eG, pG[:, 0:32], AF.Exp)
                S.act(eGt, pG[:, 32:64], AF.Exp)
                S.tt(eGr, pG[:, 32:64], Gs, ALU.subtract)
                S.act(eGr, eGr, AF.Exp)
                S.tt(bG, beta.rearrange("p s h -> p (s h)"), eG, ALU.mult)
                beta2 = beta.rearrange("p s h -> p (s h)")
                S.ts(betah, beta2, 0.5, ALU.mult)
                S.ts(betan, beta2, -1.0, ALU.mult)

                chk(1)
                chk(2)
                sstate = {"L_done": False, "Z_done": False, "zw": [None, None]}
                zsb = [t_[:, :].bitcast(BF16)[:, j * TB:(j + 1) * TB] for t_ in (tmpA[0], tmpA[1], tmpB[0], tmpB[1])
                       for j in range(2)]

                def gen_delta(bt):
                    h0 = bt * 4
                    pool = "D%d" % bt
                    need_q = full or blk == NPRE - 1
                    qk_list = []
                    for (gname, dest, cbase, c0) in (("k", kT, 8, C_K), ("v", vT, 16, C_V), ("q", qT, 0, C_Q)):
                        if gname == "q" and not need_q:
                            continue
                        wv = wload(bt, w_in_v[:, :, c0 + 512 * bt:c0 + 512 * bt + 512], 8, 512)
                        for ci in range(4):
                            h = h0 + ci
                            pb = inproj(wv, ci, xTb, pool=pool)
                            yield
                            co = conv(pb, tailq[:, cbase + h, :],
                                      chp[:, CH_CWQ + 4 * (cbase + h):CH_CWQ + 4 * (cbase + h) + 4], bt, pool)
                            yield
                            S.act(tq[bt], co[:, :], AF.Tanh, scale=0.5)
                            S.stt(dest[:, h, :], tq[bt], 1.0, co[:, :], ALU.add, ALU.mult)
                            if gname == "k" or (gname == "q" and full):
                                qk_list.append(((0 if gname == "k" else 4) + ci, dest, h, gname))
                            yield
                    if full:
                        sstate["zw"][bt] = wload(bt, w_in_v[:, :, C_Z + 512 * bt:C_Z + 512 * bt + 512], 8, 512)
                    pnorm = nb(pool)
                    for n_, (r, dest, h, gname) in enumerate(qk_list):
                        S.act(sqb[bt], dest[:, h, :], AF.Square)
                        S.mm(pnorm[0:16, :], E_bf[:, r * 16:(r + 1) * 16], sqb[bt],
                             start=(n_ == 0), stop=(n_ == len(qk_list) - 1))
                        yield
                    rs_ = rs16b[bt]
                    S.act(rs_[0:16, :], pnorm[0:16, :], AF.Ln, bias=4.0 * RMS_EPS)
                    S.act(rs_[0:16, :], rs_[0:16, :], AF.Exp, scale=-0.5)
                    yield
                    for n_, (r, dest, h, gname) in enumerate(qk_list):
                        pb = nb(pool)
                        S.mm(pb[:, :], Ep[:, r * 128:(r + 1) * 128], rs_[0:16, :])
                        if gname == "q":
                            S.stt(dest[:, h, :], dest[:, h, :], 128.0 ** -0.5, pb[:, :], ALU.mult, ALU.mult)
                        else:
                            S.tt(dest[:, h, :], dest[:, h, :], pb[:, :], ALU.mult)
                        yield
                    X, XT, RTf = xrt[:, 3 * bt + 0], xrt[:, 3 * bt + 1], xrt[:, 3 * bt + 2]
                    T1_, T2_ = XpT1[bt], T2b[bt]
                    for sub in range(4):
                        tsl = slice(sub * 128, (sub + 1) * 128)
                        sc = slice(sub * 8 + h0, sub * 8 + h0 + 4)
                        pk = nb(pool)
                        pkb = pk[:].bitcast(BF16)
                        for hh in range(4):
                            S.transpose(pkb[:, hh * 128:(hh + 1) * 128], kT[:, h0 + hh, tsl], ident_bf)
                            S.transpose(pkb[:, 512 + hh * 128:512 + (hh + 1) * 128], vT[:, h0 + hh, tsl], ident_bf)
                        pk3 = pkb[:, 0:512].rearrange("p (h d) -> p h d", h=4)
                        pv3 = pkb[:, 512:1024].rearrange("p (h d) -> p h d", h=4)
                        yield
                        S.tt(kbe[bt], pk3, bG[:, sc].unsqueeze(2).to_broadcast([128, 4, 128]), ALU.mult)
                        S.tt(kdc[bt], pk3, eGr[:, sc].unsqueeze(2).to_broadcast([128, 4, 128]), ALU.mult)
                        S.tt(vbt[bt], pv3, betah[:, sc].unsqueeze(2).to_broadcast([128, 4, 128]), ALU.mult)
                        S.tt(T1_, Umat.unsqueeze(1).to_broadcast([128, 4, 128]),
                             gg2[:, sc].unsqueeze(2).to_broadcast([128, 4, 128]), ALU.mult, eng="gpsimd")
                        yield
                        pF = nb(pool)
                        S.mm(pF[:, :], ones, T1_.rearrange("p h j -> p (h j)"))
                        pF3 = pF[:, :].rearrange("p (h j) -> p h j", h=4)
                        pkk = nb(pool)
                        pkk3 = pkk[:, :].rearrange("p (h j) -> p h j", h=4)
                        for hh in range(4):
                            S.mm(pkk3[:, hh, :], kT[:, h0 + hh, tsl], kT[:, h0 + hh, tsl])
                        yield
                        for hh in range(4):
                            gcol = Gs[:, sub * 8 + h0 + hh:sub * 8 + h0 + hh + 1]
                            S.stt(T1_[:, hh, :], pF3[:, hh, :], gcol, POSL, ALU.subtract, ALU.add)
                            S.stt(T2_[:, hh, :], pF3[:, hh, :], gcol, POSU, ALU.subtract, ALU.subtract)
                        if full:
                            S.act(eGw[bt], pF3, AF.Exp)
                        S.act(GL[bt], T1_, AF.Exp, scale=-1.0)
                        S.act(GU[bt], T2_, AF.Exp)
                        yield
                        if full:
                            S.tt(qe[bt], qT[:, h0:h0 + 4, tsl], eGw[bt], ALU.mult, eng="gpsimd")
                        S.tt(GL[bt], GL[bt], betan[:, sc].unsqueeze(2).to_broadcast([128, 4, 128]), ALU.mult, eng="gpsimd")
                        S.tt(X, pkk3, GL[bt], ALU.mult)
                        yield
                        if full:
                            pqk = nb(pool)
                            pqk3 = pqk[:, :].rearrange("p (h j) -> p h j", h=4)
                            for hh in range(4):
                                S.mm(pqk3[:, hh, :], kT[:, h0 + hh, tsl], qT[:, h0 + hh, tsl])
                            S.tt(PT[bt], pqk3, GU[bt], ALU.mult)
                            yield
                        pat = nb(pool)
                        for hh in range(4):
                            S.mm(pat[:, hh * 128:(hh + 1) * 128], X[:, hh, :], identr[:])
                        pat3 = pat[:, :].rearrange("p (h j) -> p h j", h=4)
                        S.copy("scalar", XT, pat3)
                        S.tt(RTf, pat3, ident.unsqueeze(1).to_broadcast([128, 4, 128]), ALU.add)
                        yield
                        XTRT = xrt[:, 3 * bt + 1:3 * bt + 3]
                        for lvl in range(1, 7):
                            px = nb(pool)
                            px3 = px[:, :].rearrange("p (h j) -> p h j", h=4)
                            for hh in range(4):
                                S.mm(px3[:, hh, :], XT[:, hh, :], X[:, hh, :])
                            if lvl in (1, 6):
                                pc1 = nb(pool)
                                pc13 = pc1[:, :].rearrange("p (h j) -> p h j", h=4)
                                src_ = XT if lvl == 1 else RTf
                                for hh in range(4):
                                    S.mm(pc13[:, hh, :], X[:, hh, :], src_[:, hh, :])
                                yield
                                S.copy("scalar", X, px3)
                                if lvl == 1:
                                    S.copy("vector", XT, pc13)
                                else:
                                    S.tt(RTf, pc13, RTf, ALU.add)
                                yield
                                continue
                            pcs = [nb(pool), nb(pool)]
                            pcv = [pc_[:, :].rearrange("p (h t j) -> p h t j", h=2, t=2) for pc_ in pcs]
                            for hh in range(4):
                                S.mm(pcv[hh // 2][:, hh % 2, :, :], X[:, hh, :], XTRT[:, :, hh, :])
                            yield
                            S.copy("scalar", X, px3)
                            for g2 in range(2):
                                S.copy("vector", XT[:, 2 * g2:2 * g2 + 2, :], pcv[g2][:, :, 0, :])
                                S.tt(RTf[:, 2 * g2:2 * g2 + 2, :], pcv[g2][:, :, 1, :], RTf[:, 2 * g2:2 * g2 + 2, :], ALU.add)
                            yield
                        pr = nb(pool)
                        pr3 = pr[:, :].rearrange("p (h j) -> p h j", h=4)
                        for hh in range(4):
                            S.mm(pr3[:, hh, :], X[:, hh, :], RTf[:, hh, :])
                        yield
                        S.tt(RTb[bt], pr3, RTf, ALU.add)
                        RT = RTb[bt]
                        pw = nb(pool)
                        pw3 = pw[:, :].rearrange("p (h j) -> p h j", h=4)
                        for hh in range(4):
                            S.mm(pw3[:, hh, :], kbe[bt][:, hh, :], RT[:, hh, :])
                        yield
                        S.act(nwT[bt], pw3, AF.Copy, scale=-1.0)
                        pvn = nb(pool)
                        pvn3 = pvn[:, :].rearrange("p (h j) -> p h j", h=4)
                        for hh in range(4):
                            S.mm(pvn3[:, hh, :], RT[:, hh, :], vbt[bt][:, hh, :], start=True, stop=False)
                            S.mm(pvn3[:, hh, :], nwT[bt][:, hh, :], Sbf[:, h0 + hh, :], start=False, stop=True)
                        yield
                        S.copy("vector", vnew[bt], pvn3)
                        if full:
                            po = nb(pool)
                            po3 = po[:, :].rearrange("p (h j) -> p h j", h=4)
                            for hh in range(4):
                                S.mm(po3[:, hh, :], Sbf[:, h0 + hh, :], qe[bt][:, hh, :], start=True, stop=False)
                                S.mm(po3[:, hh, :], vnew[bt][:, hh, :], PT[bt][:, hh, :], start=False, stop=True)
                            yield
                            S.copy("scalar", ogT[:, h0:h0 + 4, tsl], po3)
                        pds = nb(pool)
                        pds3 = pds[:, :].rearrange("p (h j) -> p h j", h=4)
                        for hh in range(4):
                            S.mm(pds3[:, hh, :], kdc[bt][:, hh, :], vnew[bt][:, hh, :])
                        yield
                        for hh in range(4):
                            S.stt(S32[:, h0 + hh, :], S32[:, h0 + hh, :],
                                  eGt[:, sub * 8 + h0 + hh:sub * 8 + h0 + hh + 1], pds3[:, hh, :], ALU.mult, ALU.add)
                        S.copy("scalar", Sbf[:, h0:h0 + 4, :], S32[:, h0:h0 + 4, :])
                        yield
                    if full:
                        pn = nb(pool)
                        for ci in range(4):
                            S.act(sqb[bt], ogT[:, h0 + ci, :], AF.Square)
                            S.mm(pn[0:16, :], E_bf[:, ci * 16:(ci + 1) * 16], sqb[bt], start=(ci == 0), stop=(ci == 3))
                            yield
                        rs_ = rs16b[bt]
                        S.act(rs_[0:16, :], pn[0:16, :], AF.Ln, bias=RMS_EPS, scale=1.0 / 128.0)
                        S.act(rs_[0:16, :], rs_[0:16, :], AF.Exp, scale=-0.5)
                        yield
                        while not sstate["Z_done"]:
                            yield
                        for ci in range(4):
                            h = h0 + ci
                            pb = nb(pool)
                            S.mm(pb[:, :], Ep[:, ci * 128:(ci + 1) * 128], rs_[0:16, :])
                            yield
                            t_ = tmpC if bt == 0 else tmpD
                            S.stt(t_, ogT[:, h, :], der[:, DR_NWH:DR_NWH + 1], pb[:, :], ALU.mult, ALU.mult)
                            S.tt(ogT[:, h, :], t_, zsb[h], ALU.mult)
                            yield

                def gen_z():
                    while not sstate["L_done"]:
                        yield
                    for h in range(8):
                        bt_ = h // 4
                        while sstate["zw"][bt_] is None:
                            yield
                        pz = inproj(sstate["zw"][bt_], h % 4, xTb, pool="L")
                        yield
                        S.act(tmpE, pz[:, :], AF.Tanh, scale=0.5)
                        S.stt(zsb[h], tmpE, 1.0, pz[:, :], ALU.add, ALU.mult)
                        yield
                    sstate["Z_done"] = True

                def gen_lru():
                    wa = wload(2, lru_wa.rearrange("h i j -> i h j"), HL, 128)
                    wi = wload(3, lru_wi.rearrange("h i j -> i h j"), HL, 128)
                    lxw = None
                    for hl in range(HL):
                        if hl in (0, 4, 8):
                            n_ = 512 if hl < 8 else 256
                            lxw = wload(4, w_in_v[:, :, C_LX + 128 * hl:C_LX + 128 * hl + n_], 8, n_)
                        pb = inproj(lxw, hl % 4, xTb, pool="L")
                        yield
                        pc_ = conv(pb, taill[:, hl, :], chp[:, CH_CWL + 4 * hl:CH_CWL + 4 * hl + 4], 2, "L")
                        yield
                        u = cv[2]
                        S.act(u, pc_[:, :], AF.Identity, bias=chp[:, CH_CBL + hl:CH_CBL + hl + 1])
                        S.copy("vector", ubf, u)
                        pr_ = nb("L")
                        S.mm(pr_[:, :], wa[:, hl, :], ubf)
                        pi_ = nb("L")
                        S.mm(pi_[:, :], wi[:, hl, :], ubf)
                        yield
                        th = tmpA[hl % 2]
                        S.act(th, pr_[:, :], AF.Tanh, bias=der[:, DR_HBA + hl:DR_HBA + hl + 1], scale=0.5)
                        thi = tmpB[hl % 2]
                        S.act(thi, pi_[:, :], AF.Tanh, bias=der[:, DR_HBI + hl:DR_HBI + hl + 1], scale=0.5)
                        a_ = tmpC
                        S.act(a_, th, AF.Exp, bias=der[:, DR_HLS + hl:DR_HLS + hl + 1],
                              scale=der[:, DR_HLS + hl:DR_HLS + hl + 1])
                        yield
                        ml = tmpD
                        S.act(ml, a_, AF.Square)
                        S.act(ml, ml, AF.Ln, bias=1.0, scale=-1.0)
                        S.act(ml, ml, AF.Exp, scale=0.5)
                        yield
                        first_pre = (not full) and blk == 0
                        if first_main or first_pre:
                            fcol = flag[:, 0:1] if first_main else der[:, DR_OMF:DR_OMF + 1]
                            t0_ = t0buf[:, 0:1]
                            S.ts(t0_, ml[:, 0:1], -1.0, ALU.mult, 1.0, ALU.add)
                            S.ts(t0_, t0_, fcol, ALU.mult)
                            S.tt(ml[:, 0:1], ml[:, 0:1], t0_, ALU.add)
                            if first_main:
                                S.ts(hst[:, hl:hl + 1], hst[:, hl:hl + 1], der[:, DR_OMF:DR_OMF + 1], ALU.mult)
                        bin_ = tmpE
                        S.stt(thi, thi, 1.0, u, ALU.add, ALU.mult)
                        S.stt(bin_, thi, 0.5, ml, ALU.mult, ALU.mult)
                        yield
                        hs = thi
                        S.op("vector", lambda e, hs=hs, a_=a_, bin_=bin_, hl=hl: e.tensor_tensor_scan(
                            out=hs, data0=a_, data1=bin_, initial=hst[:, hl:hl + 1], op0=ALU.mult, op1=ALU.add),
                            reads=[a_, bin_, hst[:, hl:hl + 1]], writes=[hs])
                        S.copy("vector", hst[:, hl:hl + 1], hs[:, TB - 1:TB])
                        if full:
                            S.copy("scalar", lroT[:, hl, :], hs)
                        yield
                    if full:
                        lgw = None
                        for hl in range(HL):
                            if hl in (0, 4, 8):
                                n_ = 512 if hl < 8 else 256
                                lgw = wload(4, w_in_v[:, :, C_LG + 128 * hl:C_LG + 128 * hl + n_], 8, n_)
                            pg = inproj(lgw, hl % 4, xTb, pool="L")
                            yield
                            x2 = tmpA[hl % 2]
                            S.act(x2, pg[:, :], AF.Square)
                            S.ts(x2, x2, GC1, ALU.mult, 1.0, ALU.add)
                            S.tt(x2, x2, pg[:, :], ALU.mult)
                            yield
                            S.act(x2, x2, AF.Tanh, scale=GC0)
                            gl_ = tmpB[hl % 2]
                            S.stt(gl_, x2, 1.0, pg[:, :], ALU.add, ALU.mult)
                            S.stt(lroT[:, hl, :], gl_, 0.5, lroT[:, hl, :], ALU.mult, ALU.mult)
                            yield

                def gen_lru_wrap():
                    yield from gen_lru()
                    sstate["L_done"] = True

                gens = [[gen_delta(0), 0.0], [gen_delta(1), 0.0], [gen_lru_wrap(), 0.0]]
                if full:
                    gens.append([gen_z(), 0.0])
                S._track_stream = True
                while gens:
                    gens.sort(key=lambda ge: ge[1])
                    ge = gens[0]
                    S.last_fin = 0.0
                    try:
                        next(ge[0])
                        if S.last_fin > 0.0:
                            ge[1] = S.last_fin
                        else:
                            ge[1] += 0.5
                    except StopIteration:
                        gens.remove(ge)
                S._track_stream = False

                chk(5)
                if not full:
                    continue

                ws["limit"] = (blk + 1) * n_p2
                for hf in range(2):
                    gdw = wnext("gd%d" % hf)
                    glw = wnext("gl%d" % hf, keep=1)
                    bdw = wnext("bdn%d" % hf, keep=2)
                    blw = wnext("blru%d" % hf, keep=3)
                    for ci in range(4):
                        dc = hf * 4 + ci
                        pgd = inproj(gdw, ci, xTb)
                        pgl = inproj(glw, ci, xTb)
                        pyd = nb()
                        for kc in range(8):
                            S.mm(pyd[:, :], bdw[:, kc, ci * 128:(ci + 1) * 128], ogT[:, kc, :], start=(kc == 0), stop=(kc == 7))
                        pyl = nb()
                        for kc in range(HL):
                            S.mm(pyl[:, :], blw[:, kc, ci * 128:(ci + 1) * 128], lroT[:, kc, :], start=(kc == 0), stop=(kc == HL - 1))
                        S.act(gt1, pgd[:, :], AF.Tanh, bias=der[:, DR_HBM + dc:DR_HBM + dc + 1], scale=0.5)
                        S.act(gt2, pgl[:, :], AF.Tanh, bias=der[:, DR_HBM + 8 + dc:DR_HBM + 8 + dc + 1], scale=0.5)
                        S.stt(m1, gt1, 1.0, pyd[:, :], ALU.add, ALU.mult)
                        S.stt(m2, gt2, 1.0, pyl[:, :], ALU.add, ALU.mult)
                        S.tt(mrgT[:, dc, :], m1, m2, ALU.add)
                def ln_load(which):
                    S.dma(lnp.rearrange("p a d -> p (a d)"), lnp_d[:, which * 2 * D:(which + 1) * 2 * D])

                def ln_st(st):
                    S.act(rstd[:, st:st + 1], mv[:, st, 1:2], AF.Ln, bias=LN_EPS)
                    S.act(rstd[:, st:st + 1], rstd[:, st:st + 1], AF.Exp, scale=-0.5)
                    S.ts(h1[:, st, :], h1[:, st, :], mv[:, st, 0:1], ALU.subtract, rstd[:, st:st + 1], ALU.mult)
                    S.tt(h1[:, st, :], h1[:, st, :], lnp[:, 0, :], ALU.mult)
                    S.tt(h1[:, st, :], h1[:, st, :], lnp[:, 1, :], ALU.add)

                ln_load(0)
                wo = [wnext("wo0"), wnext("wo1", keep=1)]
                for st in range(4):
                    tsl = slice(st * 128, (st + 1) * 128)
                    S.dma(xtok, x_main[blk * TB + st * 128:blk * TB + (st + 1) * 128, :])
                    for hf in range(2):
                        pm = nb()
                        for kc in range(8):
                            S.mm(pm[:, :], mrgT[:, kc, tsl], wo[hf][:, kc, :], start=(kc == 0), stop=(kc == 7))
                        csl = slice(hf * 512, (hf + 1) * 512)
                        S.act(xtok[:, csl], xtok[:, csl], AF.Copy, scale=ALPHA)
                        S.stt(h1[:, st, csl], pm[:, :], 0.5, xtok[:, csl], ALU.mult, ALU.add)
                        S.op("vector", lambda e, st=st, hf=hf, csl=csl: e.bn_stats(out=stats[:, st, hf, :], in_=h1[:, st, csl]),
                             reads=[h1[:, st, csl]], writes=[stats[:, st, hf, :]])
                    S.op("vector", lambda e, st=st: e.bn_aggr(out=mv[:, st, :], in_=stats[:, st, :, :].rearrange("p a b -> p (a b)")),
                         reads=[stats[:, st, :, :]], writes=[mv[:, st, :]])
                    ln_st(st)
                    S.copy("scalar", h1b, h1[:, st, :])
                    pt = nb()
                    ptb = pt[:].bitcast(BF16)
                    for kc in range(8):
                        S.transpose(ptb[:, kc * 128:(kc + 1) * 128], h1b[:, kc * 128:(kc + 1) * 128], ident_bf)
                    S.copy("vector", h1T[:, :, st * 128:(st + 1) * 128], ptb[:, :].rearrange("p (k t) -> p k t", k=8))

                for i in range(6):
                    wg = wnext("wg%d" % i)
                    wu = wnext("wu%d" % i, keep=1)
                    for ci in range(4 if i < 5 else 2):
                        f = i * 4 + ci
                        pg = nb()
                        for kc in range(8):
                            S.mm(pg[:, :], wg[:, kc, ci * 128:(ci + 1) * 128], h1T[:, kc, :], start=(kc == 0), stop=(kc == 7))
                        pu = nb()
                        for kc in range(8):
                            S.mm(pu[:, :], wu[:, kc, ci * 128:(ci + 1) * 128], h1T[:, kc, :], start=(kc == 0), stop=(kc == 7))
                        sg = gt1 if f % 2 == 0 else gt2
                        S.act(sg, pg[:, :], AF.Silu)
                        S.tt(actT[:, f, :], pu[:, :], sg, ALU.mult)
                ln_load(1)
                for hf in range(2):
                    csl = slice(hf * 512, (hf + 1) * 512)
                    pfs = [nb() for _ in range(4)]
                    for j, (f0, nf) in enumerate(((0, 8), (8, 8), (16, 6))):
                        wd = wnext("wd%d_%d" % (hf, j))
                        for st in range(4):
                            for ff in range(nf):
                                f = f0 + ff
                                S.mm(pfs[st][:, :], actT[:, f, st * 128:(st + 1) * 128], wd[:, ff, :],
                                     start=(f == 0), stop=(f == NFC - 1))
                    for st in range(4):
                        S.stt(h1[:, st, csl], h1[:, st, csl], ALPHA, pfs[st][:, :], ALU.mult, ALU.add)
                        S.op("vector", lambda e, st=st, hf=hf, csl=csl: e.bn_stats(out=stats[:, st, hf, :], in_=h1[:, st, csl]),
                             reads=[h1[:, st, csl]], writes=[stats[:, st, hf, :]])
                for st in range(4):
                    S.op("vector", lambda e, st=st: e.bn_aggr(out=mv[:, st, :], in_=stats[:, st, :, :].rearrange("p a b -> p (a b)")),
                         reads=[stats[:, st, :, :]], writes=[mv[:, st, :]])
                    ln_st(st)
                    S.dma(out_d[blk * TB + st * 128:blk * TB + (st + 1) * 128, :], h1[:, st, :])


          except _Stop:
            break
        assert stage != 99 or ws["used"] == len(allg), (ws["used"], len(allg))
        with nc.Block() as block:
            S.replay(block)
    S.dbg_names = dbg_names
    return nc, S


def _consts():
    c = np.zeros((128, NCST), np.float32)
    i = np.arange(128)
    c[:, CS_ID:CS_ID + 128] = np.eye(128, dtype=np.float32)
    c[:, CS_U:CS_U + 128] = (i[:, None] <= i[None, :]).astype(np.float32)
    c[:, CS_ONE:CS_ONE + 128] = 1.0
    c[:, CS_POSL:CS_POSL + 128] = np.where(i[None, :] >= i[:, None], BIG, 0.0)
    c[:, CS_POSU:CS_POSU + 128] = np.where(i[None, :] < i[:, None], BIG, 0.0)
    for r in range(16):
        c[:, CS_E + r * 16 + r] = 1.0
        c[r, CS_EP + r * 128:CS_EP + (r + 1) * 128] = 1.0
    return c


def _prep_shared(inp):
    sh = {}
    sh["w_in"] = np.ascontiguousarray(inp["w_in"][0])
    sh["lru_wa"] = np.ascontiguousarray(inp["lru_w_a"][0])
    sh["lru_wi"] = np.ascontiguousarray(inp["lru_w_i"][0])
    sh["w_bdn"] = np.ascontiguousarray(inp["w_branch_dn"][0])
    sh["w_blru"] = np.ascontiguousarray(inp["w_branch_lru"][0])
    sh["w_out"] = np.ascontiguousarray(inp["w_out"][0])
    sh["w_g"] = np.ascontiguousarray(inp["w_ffn_gate"][0])
    sh["w_u"] = np.ascontiguousarray(inp["w_ffn_up"][0])
    sh["w_d"] = np.ascontiguousarray(inp["w_ffn_down"][0])
    sh["cst_in"] = _consts()
    chp = np.zeros((128, NCHP), np.float32)
    chp[:, CH_CWQ:CH_CWQ + 96] = inp["dn_conv_w"][0].reshape(4, 24, 128).transpose(2, 1, 0).reshape(128, 96)
    chp[:, CH_CWL:CH_CWL + 40] = inp["lru_conv_w"][0].reshape(4, 10, 128).transpose(2, 1, 0).reshape(128, 40)
    chp[:, CH_CBL:CH_CBL + 10] = inp["lru_conv_b"][0].reshape(10, 128).T
    chp[:, CH_BA:CH_BA + 10] = inp["lru_b_a"][0].reshape(10, 128).T
    chp[:, CH_BI:CH_BI + 10] = inp["lru_b_i"][0].reshape(10, 128).T
    chp[:, CH_LAM:CH_LAM + 10] = inp["lru_lambda"][0].reshape(10, 128).T
    chp[:, CH_BM:CH_BM + 16] = inp["b_merge_gate"][0].reshape(16, 128).T
    chp[:, CH_NW] = inp["dn_norm_w"][0]
    sh["chp_in"] = chp
    rowp = np.zeros((128, 16), np.float32)
    rowp[:, 0:8] = inp["dn_A_log"][0][None, :]
    rowp[:, 8:16] = inp["dn_dt_bias"][0][None, :]
    sh["rowp_in"] = rowp
    lnp = np.zeros((128, 4 * D), np.float32)
    for j, k in enumerate(("ln1_g", "ln1_b", "ln2_g", "ln2_b")):
        lnp[:, j * D:(j + 1) * D] = inp[k][0][None, :]
    sh["lnp_in"] = lnp
    return sh


def kernel(**inp):
    x = np.asarray(inp["x"], np.float32)
    B, T, _ = x.shape
    half = T // 2
    nblk = half // TB
    nc, _ = build(nblk, nblk)
    sh = _prep_shared({k: np.asarray(v, np.float32) for k, v in inp.items()})
    in_maps = []
    for b in range(B):
        for hs in range(2):
            m = dict(sh)
            xm = x[b, hs * half:(hs + 1) * half]
            m["x_main"] = np.ascontiguousarray(xm)
            m["xT_main"] = np.ascontiguousarray(xm.T)
            if hs == 0:
                m["xT_pre"] = np.zeros((D, half), np.float32)
                m["flag_in"] = np.ones((128, 1), np.float32)
            else:
                m["xT_pre"] = np.ascontiguousarray(x[b, 0:half].T)
                m["flag_in"] = np.zeros((128, 1), np.float32)
            in_maps.append(m)
    res = run_bass_kernel_spmd(nc, in_maps, core_ids=list(range(2 * B)))
    out = np.empty((B, T, D), np.float32)
    for b in range(B):
        for hs in range(2):
            out[b, hs * half:(hs + 1) * half] = res.results[b * 2 + hs]["out"]
    return out
```

```python
import numpy as np
import concourse.bass as bass
import concourse.mybir as mybir
from concourse.bass_utils import run_bass_kernel_spmd
from contextlib import ExitStack

F32 = mybir.dt.float32
BF16 = mybir.dt.bfloat16
AF = mybir.ActivationFunctionType
ALU = mybir.AluOpType

F32R = mybir.dt.float32r
_DT_SIZE = {mybir.dt.float32: 4, mybir.dt.bfloat16: 2, mybir.dt.float32r: 4}

D = 1024
H = 8
HL = 10
DFF = 2816
NFC = 22
TB = 512
ALPHA = 2.0 ** 0.25
LN_EPS = 1e-5
RMS_EPS = 1e-6
BIG = 1.0e5
GC0 = 0.7978845608028654
GC1 = 0.044715
C_Q, C_K, C_V, C_Z, C_BD, C_LX, C_LG, C_GD, C_GL = 0, 1024, 2048, 3072, 4096, 4112, 5392, 6672, 7696

CS_ID, CS_U, CS_ONE, CS_POSL, CS_POSU, CS_E, CS_EP = 0, 128, 256, 384, 512, 640, 896
NCST = 896 + 2048
CH_CWQ, CH_CWL, CH_CBL, CH_BA, CH_BI, CH_LAM, CH_BM, CH_NW = 0, 96, 136, 146, 156, 166, 176, 192
NCHP = 193


def _box(ap):
    t = ap.tensor
    name = t.name
    dims = [list(d) for d in ap.ap]
    esz = _DT_SIZE[ap.dtype]
    if "dram" in str(type(t)).lower():
        lo = ap.offset
        hi = lo
        for st, cnt in dims:
            if cnt > 1:
                hi += abs(st) * (cnt - 1)
        return (name, 0, 1, lo * esz, (hi + 1) * esz)
    pstride = dims[0][0]
    pcnt = dims[0][1]
    if pstride == 0:
        p0, f0 = 0, ap.offset
    else:
        p0 = ap.offset // pstride
        f0 = ap.offset - p0 * pstride
    ext = 0
    for st, cnt in dims[1:]:
        if cnt > 1:
            ext += abs(st) * (cnt - 1)
    if name.startswith("ps") and name[2:].isdigit():
        return (name, 0, 128, 0, 2048)
    return (name, p0, p0 + pcnt, f0 * esz, (f0 + ext + 1) * esz)


def _ovl(a, b):
    return a[1] < b[2] and b[1] < a[2] and a[3] < b[4] and b[3] < a[4]


def _contains(a, b):
    return a[1] <= b[1] and a[2] >= b[2] and a[3] <= b[3] and a[4] >= b[4]


class Sched:
    ENGS = ("tensor", "vector", "scalar", "gpsimd", "sync")

    def __init__(self, nc, n_dma_sems):
        self.nc = nc
        self.q = {e: [] for e in self.ENGS}
        self.cnt = {e: 0 for e in self.ENGS}
        self.sem = {}
        self.waited = {e: {} for e in self.ENGS}
        self.n_dma = n_dma_sems
        self.dma_sems = []
        self.dma_cnt = [0] * n_dma_sems
        self.dma_rr = 0
        self.dma_rr_e = {}
        self.recs = {}
        self.semobjs = {}
        self.n_wait = 0
        self.n_ins = 0
        self.efree = {e: 0.0 for e in self.ENGS}
        self.tfin = {}
        self.last_fin = 0.0

    def _est(self, eng, deps_tokens, dur, hop=0.0):
        t = self.efree[eng]
        for tk in deps_tokens:
            t = max(t, self.tfin.get(tk, 0.0) + (hop if tk[0] != ("E", eng) else 0.0))
        t += dur
        self.efree[eng] = t
        self.last_fin = max(self.last_fin, t) if self._track_stream else self.last_fin
        return t

    _track_stream = False

    def set_sems(self, eng_sems, dma_sems):
        for e, s in eng_sems.items():
            self.sem[e] = s
            self.semobjs[("E", e)] = s
        self.dma_sems = dma_sems
        for i, s in enumerate(dma_sems):
            self.semobjs[("D", i)] = s

    def _need(self, eng, tok, deps):
        key, val = tok
        if self.waited[eng].get(key, 0) >= val:
            return
        if key == ("E", eng) and eng == "tensor":
            return
        if val > deps.get(key, 0):
            deps[key] = val

    def _collect(self, eng, reads, writes):
        deps = {}
        rb = [_box(a) for a in reads]
        wb = [_box(a) for a in writes]
        for b in rb:
            rec = self.recs.get(b[0])
            if rec is None:
                continue
            for (ob, tok) in rec["w"]:
                if _ovl(b, ob):
                    self._need(eng, tok, deps)
            if b[0].startswith("ps") and b[0][2:].isdigit():
                for (ob, tok) in rec["r"]:
                    if tok[0] != ("E", eng):
                        self._need(eng, tok, deps)
        for b in wb:
            rec = self.recs.get(b[0])
            if rec is None:
                continue
            for (ob, tok) in rec["w"]:
                if _ovl(b, ob):
                    self._need(eng, tok, deps)
            for (ob, tok) in rec["r"]:
                if _ovl(b, ob):
                    self._need(eng, tok, deps)
        return deps, rb, wb

    def _record(self, tok, rb, wb):
        for b in wb:
            rec = self.recs.setdefault(b[0], {"w": [], "r": []})
            rec["w"] = [(ob, t) for (ob, t) in rec["w"] if not _contains(b, ob)]
            rec["r"] = [(ob, t) for (ob, t) in rec["r"] if not _contains(b, ob)]
            rec["w"].append((b, tok))
        for b in rb:
            rec = self.recs.setdefault(b[0], {"w": [], "r": []})
            rec["r"] = [(ob, t) for (ob, t) in rec["r"]
                        if not (t[0] == tok[0] and _contains(b, ob))]
            rec["r"].append((b, tok))

    def _emit_waits(self, eng, deps):
        for key, val in deps.items():
            self.waited[eng][key] = val
            s = self.semobjs[key]
            self.q[eng].append(lambda e, s=s, val=val: e.wait_ge(s, val))
            self.n_wait += 1

    def _dep_tokens(self, rb, wb):
        toks = []
        for b in rb:
            rec = self.recs.get(b[0])
            if rec:
                toks += [t for (ob, t) in rec["w"] if _ovl(b, ob)]
        for b in wb:
            rec = self.recs.get(b[0])
            if rec:
                toks += [t for (ob, t) in rec["w"] if _ovl(b, ob)]
                toks += [t for (ob, t) in rec["r"] if _ovl(b, ob)]
        return toks

    def op(self, eng, fn, reads=(), writes=(), dur=None):
        deps, rb, wb = self._collect(eng, reads, writes)
        dtoks = self._dep_tokens(rb, wb)
        self._emit_waits(eng, deps)
        self.cnt[eng] += 1
        idx = self.cnt[eng]
        s = self.sem[eng]
        self.q[eng].append(lambda e, fn=fn, s=s: fn(e).then_inc(s, 1))
        tok = (("E", eng), idx)
        if dur is None:
            n = 1
            if writes:
                for st_, cnt_ in list(writes[0].ap)[1:]:
                    n *= cnt_
            if eng == "tensor":
                dt_in = reads[0].dtype if reads else BF16
                passes = 4 if dt_in == F32 else (2 if dt_in == F32R else 1)
                dur = max(0.06, n / 1200.0 * passes) + 0.02
            elif eng == "scalar":
                dur = 0.22 + n / 1200.0
            elif eng == "gpsimd":
                dur = 0.25 + n / 520.0
            else:
                dur = 0.10 + n / 900.0
        self.tfin[tok] = self._est(eng, dtoks, dur, hop=0.2)
        self._record(tok, rb, wb)
        self.n_ins += 1
        return tok

    def dma(self, out, in_, eng="sync"):
        half = self.n_dma // 2
        base = 0 if eng == "sync" else half
        rr = self.dma_rr_e.get(eng, 0)
        self.dma_rr_e[eng] = (rr + 1) % half
        k = base + rr
        deps, rb, wb = self._collect(eng, [in_], [out])
        prev = self.dma_cnt[k] * 16
        if prev > 0 and self.waited[eng].get(("D", k), 0) < prev:
            deps[("D", k)] = max(deps.get(("D", k), 0), prev)
        dtoks = self._dep_tokens(rb, wb)
        self._emit_waits(eng, deps)
        self.dma_cnt[k] += 1
        val = self.dma_cnt[k] * 16
        s = self.dma_sems[k]
        self.q[eng].append(lambda e, s=s, out=out, in_=in_: e.dma_start(out=out, in_=in_).then_inc(s, 16))
        tok = (("D", k), val)
        t0 = self.efree[eng]
        for tk in dtoks:
            t0 = max(t0, self.tfin.get(tk, 0.0))
        self.efree[eng] = t0 + 0.65
        self.tfin[tok] = t0 + 3.0
        self._record(tok, rb, wb)
        self.n_ins += 1
        return tok

    def mm(self, out, lhsT, rhs, start=True, stop=True):
        return self.op("tensor", lambda e: e.matmul(out, lhsT, rhs, start=start, stop=stop),
                       reads=[lhsT, rhs], writes=[out])

    def transpose(self, out, in_, ident):
        return self.op("tensor", lambda e: e.transpose(out, in_, ident), reads=[in_, ident], writes=[out])

    def act(self, out, in_, func, bias=None, scale=1.0):
        reads = [in_]
        kw = {}
        if bias is not None:
            kw["bias"] = bias
            if not isinstance(bias, (int, float)):
                reads.append(bias)
        if not isinstance(scale, (int, float)):
            reads.append(scale)
        return self.op("scalar", lambda e: e.activation(out=out, in_=in_, func=func, scale=scale, **kw),
                       reads=reads, writes=[out])

    def copy(self, eng, out, in_):
        if eng == "scalar":
            return self.act(out, in_, AF.Copy)
        return self.op(eng, lambda e: e.tensor_copy(out=out, in_=in_), reads=[in_], writes=[out])

    def tt(self, out, in0, in1, op, eng="vector"):
        return self.op(eng, lambda e: e.tensor_tensor(out=out, in0=in0, in1=in1, op=op),
                       reads=[in0, in1], writes=[out])

    def ts(self, out, in0, s1, op0, s2=None, op1=None, eng="vector"):
        reads = [in0]
        if not isinstance(s1, (int, float)):
            reads.append(s1)
        if s2 is not None and not isinstance(s2, (int, float)):
            reads.append(s2)
        if op1 is None:
            return self.op(eng, lambda e: e.tensor_scalar(out=out, in0=in0, scalar1=s1, scalar2=None, op0=op0),
                           reads=reads, writes=[out])
        return self.op(eng, lambda e: e.tensor_scalar(out=out, in0=in0, scalar1=s1, scalar2=s2,
                                                      op0=op0, op1=op1), reads=reads, writes=[out])

    def stt(self, out, in0, scalar, in1, op0, op1):
        reads = [in0, in1]
        if not isinstance(scalar, (int, float)):
            reads.append(scalar)
        return self.op("vector", lambda e: e.scalar_tensor_tensor(out=out, in0=in0, scalar=scalar, in1=in1,
                                                                  op0=op0, op1=op1),
                       reads=reads, writes=[out])

    def memset(self, eng, ap, val):
        return self.op(eng, lambda e: e.memset(ap, val), reads=[], writes=[ap])

    def replay(self, block):
        finals = []
        for e in self.ENGS:
            if self.cnt[e] > 0:
                finals.append((self.sem[e], self.cnt[e]))
        for k in range(self.n_dma):
            if self.dma_cnt[k] > 0:
                finals.append((self.dma_sems[k], self.dma_cnt[k] * 16))
        qs = self.q

        def run(engname):
            def body(e):
                for f in qs[engname]:
                    f(e)
                if engname == "sync":
                    for s, v in finals:
                        e.wait_ge(s, v)
            return body
        block.sync(run("sync"))
        block.tensor(run("tensor"))
        block.vector(run("vector"))
        block.scalar(run("scalar"))
        block.gpsimd(run("gpsimd"))


class _Stop(Exception):
    pass


def build(NPRE, NMAIN, debug=False, stage=99):
    nc = bass.Bass("TRN2", target_bir_lowering=False)
    dbg_names = []
    TPRE, TMAIN = TB * NPRE, TB * NMAIN

    def dram(name, shape, kind="ExternalInput"):
        return nc.dram_tensor(name, shape, F32, kind=kind).ap()
    xT_pre = dram("xT_pre", [D, TPRE])
    xT_main = dram("xT_main", [D, TMAIN])
    x_main = dram("x_main", [TMAIN, D])
    w_in = dram("w_in", [D, 8720])
    lru_wa = dram("lru_wa", [HL, 128, 128])
    lru_wi = dram("lru_wi", [HL, 128, 128])
    w_bdn = dram("w_bdn", [1024, D])
    w_blru = dram("w_blru", [1280, D])
    w_out = dram("w_out", [D, D])
    w_g = dram("w_g", [D, DFF])
    w_u = dram("w_u", [D, DFF])
    w_d = dram("w_d", [DFF, D])
    cst_d = dram("cst_in", [128, NCST])
    chp_d = dram("chp_in", [128, NCHP])
    rowp_d = dram("rowp_in", [128, 16])
    lnp_d = dram("lnp_in", [128, 4 * D])
    flag_d = dram("flag_in", [128, 1])
    out_d = dram("out", [TMAIN, D], kind="ExternalOutput")

    with ExitStack() as es:
        def sb(name, shape, dt):
            return es.enter_context(nc.sbuf_tensor(name, shape, dt))
        S = Sched(nc, n_dma_sems=20)
        eng_sems = {e: es.enter_context(nc.semaphore("sem_" + e)) for e in Sched.ENGS}
        dma_sems = [es.enter_context(nc.semaphore("dsem%d" % i)) for i in range(20)]
        S.set_sems(eng_sems, dma_sems)

        cst = sb("cst", [128, NCST], F32)
        chp = sb("chp", [128, NCHP], F32)
        rowp = sb("rowp", [128, 16], F32)
        flag = sb("flag", [128, 2], F32)
        der = sb("der", [128, 64], F32)
        cbf = sb("cbf", [128, 128 + 128 + 256], BF16)
        identr = sb("identr", [128, 128], F32R)
        wbd = sb("wbd", [128, 8, 16], BF16)
        xT = [sb("xT%d" % i, [128, 8, TB], BF16) for i in range(2)]
        wslot = [sb("wslot%d" % i, [128, 5120], BF16) for i in range(5)]
        qT = sb("qT", [128, 8, TB], BF16)
        kT = sb("kT", [128, 8, TB], BF16)
        vT = sb("vT", [128, 8, TB], BF16)
        ogT = sb("ogT", [128, 8, TB], BF16)
        lroT = sb("lroT", [128, HL, TB], BF16)
        S32 = sb("S32", [128, 8, 128], F32)
        Sbf = sb("Sbf", [128, 8, 128], BF16)
        hst = sb("hst", [128, HL], F32)
        tailq = sb("tailq", [128, 24, 4], BF16)
        taill = sb("taill", [128, HL, 4], BF16)
        SCR_BYTES = 66 * 1024
        scr = sb("scr", [128, SCR_BYTES // 4], F32)
        ps = [es.enter_context(nc.psum_tensor("ps%d" % i, [128, 512], F32)) for i in range(8)]
        pstate = {"i": 0}

        pools = {"all": list(range(8)), "D0": [0, 1, 2], "D1": [3, 4, 5], "L": [6, 7]}
        pidx = {"all": 0, "D0": 0, "D1": 0, "L": 0}

        def nb(pool="all"):
            lst = pools[pool]
            i = pidx[pool]
            pidx[pool] = (i + 1) % len(lst)
            return ps[lst[i]]

        evs = {"i": 0}

        def ev():
            evs["i"] ^= 1
            return "scalar" if evs["i"] else "vector"

        def carve_factory():
            st = {"off": 0}

            def carve(shape, dt):
                n = 1
                for s_ in shape:
                    n *= s_
                nbytes = n * _DT_SIZE[dt]
                nbytes = (nbytes + 3) // 4 * 4
                off = st["off"]
                st["off"] += nbytes
                assert st["off"] <= SCR_BYTES, ("scratch overflow", st["off"])
                v = scr[:, off // 4:(off + nbytes) // 4]
                if dt == BF16:
                    v = v.bitcast(BF16)
                    v = v[:, 0:n]
                elif dt == F32R:
                    v = v.bitcast(F32R)
                if len(shape) == 2:
                    return v.rearrange("p (a b) -> p a b", a=shape[0])
                if len(shape) == 3:
                    return v.rearrange("p (a b c) -> p a b c", a=shape[0], b=shape[1])
                return v
            return carve

        c1 = carve_factory()
        xin = [c1([TB + 4], BF16) for _ in range(3)]
        dgw = [c1([4, 128], BF16) for _ in range(3)]
        cv = [None, None, c1([TB], F32)]
        tq = [c1([TB], F32) for _ in range(2)]
        betah = c1([32], F32)
        betan = c1([32], F32)
        t0buf = c1([2], F32)
        tmpA = [c1([TB], F32) for _ in range(2)]
        tmpB = [c1([TB], F32) for _ in range(2)]
        tmpC = c1([TB], F32)
        tmpD = c1([TB], F32)
        tmpE = c1([TB], F32)
        ubf = c1([TB], BF16)
        sqb = [c1([TB], BF16) for _ in range(2)]
        rs16b = [c1([TB], F32) for _ in range(2)]
        rs16 = rs16b[0]
        bdv = c1([4, 16], F32)
        beta = c1([4, 8], F32)
        gg = c1([4, 8], F32)
        tsm = [c1([4, 8], F32) for _ in range(4)]
        Gs = c1([32], F32)
        eG = c1([32], F32)
        eGt = c1([32], F32)
        eGr = c1([32], F32)
        bG = c1([32], F32)
        XpT1 = [c1([4, 128], F32) for _ in range(2)]
        T2b = [c1([4, 128], F32) for _ in range(2)]
        GL = [c1([4, 128], BF16) for _ in range(2)]
        GU = [c1([4, 128], BF16) for _ in range(2)]
        eGw = [c1([4, 128], BF16) for _ in range(2)]
        xrt = sb("xrt", [128, 6, 4, 128], F32R)
        Xa = [[xrt[:, 0], xrt[:, 1]]]
        XTa = [[xrt[:, 2], xrt[:, 3]]]
        RTa = [[xrt[:, 4], xrt[:, 5]]]
        Xa = Xa * 2; XTa = XTa * 2; RTa = RTa * 2
        RTb = [c1([4, 128], BF16) for _ in range(2)]
        PT = [c1([4, 128], BF16) for _ in range(2)]
        nwT = [c1([4, 128], BF16) for _ in range(2)]
        vnew = [c1([4, 128], BF16) for _ in range(2)]
        qe = [c1([4, 128], BF16) for _ in range(2)]
        kbe = [c1([4, 128], BF16) for _ in range(2)]
        kdc = [c1([4, 128], BF16) for _ in range(2)]
        vbt = [c1([4, 128], BF16) for _ in range(2)]
        c2 = carve_factory()
        h1 = c2([4, D], F32)
        h1T = qT[:]
        actT = c2([NFC, TB], BF16)
        mrgT = kT[:]
        lnp = c2([2, D], F32)
        h1b = c2([D], BF16)
        gt1 = c2([TB], F32)
        gt2 = c2([TB], F32)
        xtok = c2([D], F32)
        m1 = xtok[:, 0:TB]
        m2 = xtok[:, TB:2 * TB]
        stats = c2([4, 2, 6], F32)
        mv = c2([4, 2], F32)
        rstd = c2([4], F32)

        ident = cst[:, CS_ID:CS_ID + 128]
        Umat = cst[:, CS_U:CS_U + 128]
        ones = cst[:, CS_ONE:CS_ONE + 128]
        POSL = cst[:, CS_POSL:CS_POSL + 128]
        POSU = cst[:, CS_POSU:CS_POSU + 128]
        Esel = cst[:, CS_E:CS_E + 256]
        Ep = cst[0:16, CS_EP:CS_EP + 2048]
        ident_bf = cbf[:, 0:128]
        ones_bf = cbf[:, 128:256]
        E_bf = cbf[:, 256:512]
        DR_HBA, DR_HBI, DR_HLS, DR_HBM, DR_OMF, DR_NWH = 0, 10, 20, 30, 46, 47
        negA = rowp[:, 0:8]
        dtb = rowp[:, 8:16]

        S.dma(cst[:], cst_d)
        S.dma(chp[:], chp_d)
        S.dma(rowp[:], rowp_d)
        S.dma(flag[:, 0:1], flag_d)
        S.copy("vector", cbf[:, 0:128], ident)
        S.copy("vector", identr[:], ident)
        S.copy("vector", cbf[:, 128:256], ones)
        S.copy("vector", cbf[:, 256:512], Esel)
        S.act(negA, negA, AF.Exp)
        S.ts(negA, negA, -1.0, ALU.mult)
        S.ts(der[:, DR_HBA:DR_HBA + 10], chp[:, CH_BA:CH_BA + 10], 0.5, ALU.mult)
        S.ts(der[:, DR_HBI:DR_HBI + 10], chp[:, CH_BI:CH_BI + 10], 0.5, ALU.mult)
        S.ts(der[:, DR_HBM:DR_HBM + 16], chp[:, CH_BM:CH_BM + 16], 0.5, ALU.mult)
        S.act(der[:, DR_HLS:DR_HLS + 10], chp[:, CH_LAM:CH_LAM + 10], AF.Exp, scale=-1.0)
        S.act(der[:, DR_HLS:DR_HLS + 10], der[:, DR_HLS:DR_HLS + 10], AF.Ln, bias=1.0)
        S.ts(der[:, DR_HLS:DR_HLS + 10], der[:, DR_HLS:DR_HLS + 10], -4.0, ALU.mult)
        S.ts(der[:, DR_OMF:DR_OMF + 1], flag[:, 0:1], -1.0, ALU.mult, 1.0, ALU.add)
        S.ts(der[:, DR_NWH:DR_NWH + 1], chp[:, CH_NW:CH_NW + 1], 0.5, ALU.mult)
        S.memset("vector", S32[:], 0.0)
        S.memset("vector", Sbf[:], 0.0)
        S.memset("vector", hst[:], 0.0)
        S.memset("vector", tailq[:], 0.0)
        S.memset("vector", taill[:], 0.0)

        w_in_v = w_in.rearrange("(kc p) c -> p kc c", p=128)

        def g_win(name, c0, n):
            return (name, w_in_v[:, :, c0:c0 + n], 8, n)

        def block_groups(full, need_q=False):
            g = []
            if full:
                bdn_v = w_bdn.rearrange("(kc p) c -> p kc c", p=128)
                blru_v = w_blru.rearrange("(kc p) c -> p kc c", p=128)
                wo_v = w_out.rearrange("(kc p) c -> p kc c", p=128)
                wg_v = w_g.rearrange("(kc p) c -> p kc c", p=128)
                wu_v = w_u.rearrange("(kc p) c -> p kc c", p=128)
                wd_v = w_d.rearrange("(fc p) c -> p fc c", p=128)
                for hf in range(2):
                    g += [g_win("gd%d" % hf, C_GD + 512 * hf, 512), g_win("gl%d" % hf, C_GL + 512 * hf, 512),
                          ("bdn%d" % hf, bdn_v[:, :, 512 * hf:512 * hf + 512], 8, 512),
                          ("blru%d" % hf, blru_v[:, :, 512 * hf:512 * hf + 512], 10, 512)]
                g += [("wo0", wo_v[:, :, 0:512], 8, 512), ("wo1", wo_v[:, :, 512:1024], 8, 512)]
                for i in range(6):
                    n = 512 if i < 5 else 256
                    g += [("wg%d" % i, wg_v[:, :, 512 * i:512 * i + n], 8, n),
                          ("wu%d" % i, wu_v[:, :, 512 * i:512 * i + n], 8, n)]
                for hf in range(2):
                    for j, (f0, nf) in enumerate(((0, 8), (8, 8), (16, 6))):
                        g += [("wd%d_%d" % (hf, j), wd_v[:, f0:f0 + nf, 512 * hf:512 * hf + 512], nf, 512)]
            return g

        allg = []
        for b in range(NPRE):
            allg += block_groups(False, need_q=(b == NPRE - 1))
        for b in range(NMAIN):
            allg += block_groups(True)
        ws = {"issued": 0, "used": 0, "limit": 0}
        PF = 3
        n_p2 = len(block_groups(True))
        for kc in range(8):
            S.dma(wbd[:, kc, :], w_in_v[:, kc, C_BD:C_BD + 16], eng="gpsimd")

        def wload(slot_i, src, a, bb):
            dst = wslot[slot_i][:, 0:a * bb].rearrange("p (a b) -> p a b", a=a)
            for j in range(a):
                S.dma(dst[:, j, :], src[:, j, :], eng="gpsimd")
            return dst

        def w_issue_upto(n):
            while ws["issued"] < min(n, len(allg), ws["limit"]):
                i = ws["issued"]
                name, src, a, bb = allg[i]
                slot = wslot[(i % n_p2 + 2) % 5]
                dst = slot[:, 0:a * bb].rearrange("p (a b) -> p a b", a=a)
                for j in range(a):
                    S.dma(dst[:, j, :], src[:, j, :], eng="gpsimd")
                ws["issued"] += 1

        def wnext(name, keep=0):
            i = ws["used"]
            assert allg[i][0] == name, (allg[i][0], name)
            assert ws["issued"] <= (i - keep) + 5
            w_issue_upto(min(i + 1 + PF, (i - keep) + 5))
            assert ws["issued"] > i
            ws["used"] += 1
            _, _, a, bb = allg[i]
            return wslot[(i % n_p2 + 2) % 5][:, 0:a * bb].rearrange("p (a b) -> p a b", a=a)

        blocks = [("pre", i) for i in range(NPRE)] + [("main", i) for i in range(NMAIN)]

        def x_load(bi):
            kind, i = blocks[bi]
            src = (xT_pre if kind == "pre" else xT_main).rearrange("(kc p) t -> p kc t", p=128)
            for kc in range(8):
                S.dma(xT[bi % 2][:, kc, :], src[:, kc, i * TB:(i + 1) * TB], eng="gpsimd")

        x_load(0)

        def dbgdump(name, ap, n=TB):
            if not debug:
                return
            t = nc.dram_tensor("dbg_" + name, [128, n], F32, kind="ExternalOutput").ap()
            dbg_names.append("dbg_" + name)
            S.dma(t, ap)

        def inproj(wv, ci, xTb, n=TB, t0=0, pool="all"):
            pb = nb(pool)
            for kc in range(8):
                S.mm(pb[:, 0:n], wv[:, kc, ci * 128:(ci + 1) * 128], xTb[:, kc, t0:t0 + n],
                     start=(kc == 0), stop=(kc == 7))
            return pb

        def conv(pb, tail, cw, idx, pool):
            xi = xin[idx]
            dg = dgw[idx]
            for k in range(4):
                S.ts(dg[:, k, :], ident_bf, cw[:, k:k + 1], ALU.mult)
            S.copy("gpsimd", xi[:, 0:3], tail[:, 0:3])
            S.copy("scalar", xi[:, 3:TB + 3], pb[:, :])
            S.copy("gpsimd", tail[:, 0:3], xi[:, TB:TB + 3])
            pc = nb(pool)
            for k in range(4):
                S.mm(pc[:, :], dg[:, k, :], xi[:, k:k + TB], start=(k == 0), stop=(k == 3))
            return pc

        def chk(k):
            if stage == k:
                raise _Stop()

        for bi, (kind, blk) in enumerate(blocks):
          try:
                full = kind == "main"
                xTb = xT[bi % 2]
                if bi + 1 < len(blocks):
                    x_load(bi + 1)
                first_main = full and blk == 0

                wv = wbd
                pbd = nb()
                for sub in range(4):
                    for kc in range(8):
                        S.mm(pbd[:, sub * 16:(sub + 1) * 16], xTb[:, kc, sub * 128:(sub + 1) * 128], wv[:, kc, :],
                             start=(kc == 0), stop=(kc == 7))
                pbd3 = pbd[:, 0:64].rearrange("p (s c) -> p s c", s=4)
                S.act(beta, pbd3[:, :, 0:8], AF.Tanh, scale=0.5)
                S.ts(beta, beta, 0.5, ALU.mult, 0.5, ALU.add)
                xa, ab, ex, ln_ = tsm
                S.tt(xa, pbd3[:, :, 8:16], dtb.unsqueeze(1).to_broadcast([128, 4, 8]), ALU.add)
                S.ts(ab, xa, -1.0, ALU.mult)
                S.tt(ab, ab, xa, ALU.max)
                S.act(ex, ab, AF.Exp, scale=-1.0)
                S.act(ln_, ex, AF.Ln, bias=1.0)
                S.ts(xa, xa, 0.0, ALU.max)
                S.tt(xa, xa, ln_, ALU.add)
                S.tt(gg, xa, negA.unsqueeze(1).to_broadcast([128, 4, 8]), ALU.mult)
                gg2 = gg.rearrange("p s h -> p (s h)")
                pG = nb()
                S.mm(pG[:, 0:32], Umat, gg2)
                S.mm(pG[:, 32:64], ones, gg2)
                S.copy("vector", Gs, pG[:, 0:32])
                S.act(eG, pG[:, 0:32], AF.Exp)
                S.act(eGt, pG[:, 32:64], AF.Exp)
                S.tt(eGr, pG[:, 32:64], Gs, ALU.subtract)
                S.act(eGr, eGr, AF.Exp)
                S.tt(bG, beta.rearrange("p s h -> p (s h)"), eG, ALU.mult)
                beta2 = beta.rearrange("p s h -> p (s h)")
                S.ts(betah, beta2, 0.5, ALU.mult)
                S.ts(betan, beta2, -1.0, ALU.mult)

                chk(1)
                chk(2)
                sstate = {"L_done": False, "Z_done": False, "zw": [None, None]}
                zsb = [t_[:, :].bitcast(BF16)[:, j * TB:(j + 1) * TB] for t_ in (tmpA[0], tmpA[1], tmpB[0], tmpB[1])
                       for j in range(2)]

                def gen_delta(bt):
                    h0 = bt * 4
                    pool = "D%d" % bt
                    need_q = full or blk == NPRE - 1
                    qk_list = []
                    for (gname, dest, cbase, c0) in (("k", kT, 8, C_K), ("v", vT, 16, C_V), ("q", qT, 0, C_Q)):
                        if gname == "q" and not need_q:
                            continue
                        wv = wload(bt, w_in_v[:, :, c0 + 512 * bt:c0 + 512 * bt + 512], 8, 512)
                        for ci in range(4):
                            h = h0 + ci
                            pb = inproj(wv, ci, xTb, pool=pool)
                            yield
                            co = conv(pb, tailq[:, cbase + h, :],
                                      chp[:, CH_CWQ + 4 * (cbase + h):CH_CWQ + 4 * (cbase + h) + 4], bt, pool)
                            yield
                            S.act(tq[bt], co[:, :], AF.Tanh, scale=0.5)
                            S.stt(dest[:, h, :], tq[bt], 1.0, co[:, :], ALU.add, ALU.mult)
                            if gname == "k" or (gname == "q" and full):
                                qk_list.append(((0 if gname == "k" else 4) + ci, dest, h, gname))
                            yield
                    if full:
                        sstate["zw"][bt] = wload(bt, w_in_v[:, :, C_Z + 512 * bt:C_Z + 512 * bt + 512], 8, 512)
                    pnorm = nb(pool)
                    for n_, (r, dest, h, gname) in enumerate(qk_list):
                        S.act(sqb[bt], dest[:, h, :], AF.Square)
                        S.mm(pnorm[0:16, :], E_bf[:, r * 16:(r + 1) * 16], sqb[bt],
                             start=(n_ == 0), stop=(n_ == len(qk_list) - 1))
                        yield
                    rs_ = rs16b[bt]
                    S.act(rs_[0:16, :], pnorm[0:16, :], AF.Ln, bias=4.0 * RMS_EPS)
                    S.act(rs_[0:16, :], rs_[0:16, :], AF.Exp, scale=-0.5)
                    yield
                    for n_, (r, dest, h, gname) in enumerate(qk_list):
                        pb = nb(pool)
                        S.mm(pb[:, :], Ep[:, r * 128:(r + 1) * 128], rs_[0:16, :])
                        if gname == "q":
                            S.stt(dest[:, h, :], dest[:, h, :], 128.0 ** -0.5, pb[:, :], ALU.mult, ALU.mult)
                        else:
                            S.tt(dest[:, h, :], dest[:, h, :], pb[:, :], ALU.mult)
                        yield
                    X, XT, RTf = xrt[:, 3 * bt + 0], xrt[:, 3 * bt + 1], xrt[:, 3 * bt + 2]
                    T1_, T2_ = XpT1[bt], T2b[bt]
                    for sub in range(4):
                        tsl = slice(sub * 128, (sub + 1) * 128)
                        sc = slice(sub * 8 + h0, sub * 8 + h0 + 4)
                        pk = nb(pool)
                        pkb = pk[:].bitcast(BF16)
                        for hh in range(4):
                            S.transpose(pkb[:, hh * 128:(hh + 1) * 128], kT[:, h0 + hh, tsl], ident_bf)
                            S.transpose(pkb[:, 512 + hh * 128:512 + (hh + 1) * 128], vT[:, h0 + hh, tsl], ident_bf)
                        pk3 = pkb[:, 0:512].rearrange("p (h d) -> p h d", h=4)
                        pv3 = pkb[:, 512:1024].rearrange("p (h d) -> p h d", h=4)
                        yield
                        S.tt(kbe[bt], pk3, bG[:, sc].unsqueeze(2).to_broadcast([128, 4, 128]), ALU.mult)
                        S.tt(kdc[bt], pk3, eGr[:, sc].unsqueeze(2).to_broadcast([128, 4, 128]), ALU.mult)
                        S.tt(vbt[bt], pv3, betah[:, sc].unsqueeze(2).to_broadcast([128, 4, 128]), ALU.mult)
                        S.tt(T1_, Umat.unsqueeze(1).to_broadcast([128, 4, 128]),
                             gg2[:, sc].unsqueeze(2).to_broadcast([128, 4, 128]), ALU.mult, eng="gpsimd")
                        yield
                        pF = nb(pool)
                        S.mm(pF[:, :], ones, T1_.rearrange("p h j -> p (h j)"))
                        pF3 = pF[:, :].rearrange("p (h j) -> p h j", h=4)
                        pkk = nb(pool)
                        pkk3 = pkk[:, :].rearrange("p (h j) -> p h j", h=4)
                        for hh in range(4):
                            S.mm(pkk3[:, hh, :], kT[:, h0 + hh, tsl], kT[:, h0 + hh, tsl])
                        yield
                        for hh in range(4):
                            gcol = Gs[:, sub * 8 + h0 + hh:sub * 8 + h0 + hh + 1]
                            S.stt(T1_[:, hh, :], pF3[:, hh, :], gcol, POSL, ALU.subtract, ALU.add)
                            S.stt(T2_[:, hh, :], pF3[:, hh, :], gcol, POSU, ALU.subtract, ALU.subtract)
                        if full:
                            S.act(eGw[bt], pF3, AF.Exp)
                        S.act(GL[bt], T1_, AF.Exp, scale=-1.0)
                        S.act(GU[bt], T2_, AF.Exp)
                        yield
                        if full:
                            S.tt(qe[bt], qT[:, h0:h0 + 4, tsl], eGw[bt], ALU.mult, eng="gpsimd")
                        S.tt(GL[bt], GL[bt], betan[:, sc].unsqueeze(2).to_broadcast([128, 4, 128]), ALU.mult, eng="gpsimd")
                        S.tt(X, pkk3, GL[bt], ALU.mult)
                        yield
                        if full:
                            pqk = nb(pool)
                            pqk3 = pqk[:, :].rearrange("p (h j) -> p h j", h=4)
                            for hh in range(4):
                                S.mm(pqk3[:, hh, :], kT[:, h0 + hh, tsl], qT[:, h0 + hh, tsl])
                            S.tt(PT[bt], pqk3, GU[bt], ALU.mult)
                            yield
                        pat = nb(pool)
                        for hh in range(4):
                            S.mm(pat[:, hh * 128:(hh + 1) * 128], X[:, hh, :], identr[:])
                        pat3 = pat[:, :].rearrange("p (h j) -> p h j", h=4)
                        S.copy("scalar", XT, pat3)
                        S.copy("vector", RTf, ident.unsqueeze(1).to_broadcast([128, 4, 128]))
                        yield
                        XTRT = xrt[:, 3 * bt + 1:3 * bt + 3]
                        for lvl in range(1, 7):
                            px = nb(pool)
                            px3 = px[:, :].rearrange("p (h j) -> p h j", h=4)
                            for hh in range(4):
                                S.mm(px3[:, hh, :], XT[:, hh, :], X[:, hh, :])
                            pcs = [nb(pool), nb(pool)]
                            pcv = [pc_[:, :].rearrange("p (h t j) -> p h t j", h=2, t=2) for pc_ in pcs]
                            for hh in range(4):
                                S.mm(pcv[hh // 2][:, hh % 2, :, :], X[:, hh, :], XTRT[:, :, hh, :])
                            yield
                            S.copy("scalar", X, px3)
                            for g2 in range(2):
                                S.copy("vector", XT[:, 2 * g2:2 * g2 + 2, :], pcv[g2][:, :, 0, :])
                                S.tt(RTf[:, 2 * g2:2 * g2 + 2, :], pcv[g2][:, :, 1, :], RTf[:, 2 * g2:2 * g2 + 2, :], ALU.add)
                            yield
                        pr = nb(pool)
                        pr3 = pr[:, :].rearrange("p (h j) -> p h j", h=4)
                        for hh in range(4):
                            S.mm(pr3[:, hh, :], X[:, hh, :], RTf[:, hh, :])
                        yield
                        S.tt(RTb[bt], pr3, RTf, ALU.add)
                        RT = RTb[bt]
                        pw = nb(pool)
                        pw3 = pw[:, :].rearrange("p (h j) -> p h j", h=4)
                        for hh in range(4):
                            S.mm(pw3[:, hh, :], kbe[bt][:, hh, :], RT[:, hh, :])
                        yield
                        S.act(nwT[bt], pw3, AF.Copy, scale=-1.0)
                        pvn = nb(pool)
                        pvn3 = pvn[:, :].rearrange("p (h j) -> p h j", h=4)
                        for hh in range(4):
                            S.mm(pvn3[:, hh, :], RT[:, hh, :], vbt[bt][:, hh, :], start=True, stop=False)
                            S.mm(pvn3[:, hh, :], nwT[bt][:, hh, :], Sbf[:, h0 + hh, :], start=False, stop=True)
                        yield
                        S.copy("vector", vnew[bt], pvn3)
                        if full:
                            po = nb(pool)
                            po3 = po[:, :].rearrange("p (h j) -> p h j", h=4)
                            for hh in range(4):
                                S.mm(po3[:, hh, :], Sbf[:, h0 + hh, :], qe[bt][:, hh, :], start=True, stop=False)
                                S.mm(po3[:, hh, :], vnew[bt][:, hh, :], PT[bt][:, hh, :], start=False, stop=True)
                            yield
                            S.copy("scalar", ogT[:, h0:h0 + 4, tsl], po3)
                        pds = nb(pool)
                        pds3 = pds[:, :].rearrange("p (h j) -> p h j", h=4)
                        for hh in range(4):
                            S.mm(pds3[:, hh, :], kdc[bt][:, hh, :], vnew[bt][:, hh, :])
                        yield
                        for hh in range(4):
                            S.stt(S32[:, h0 + hh, :], S32[:, h0 + hh, :],
                                  eGt[:, sub * 8 + h0 + hh:sub * 8 + h0 + hh + 1], pds3[:, hh, :], ALU.mult, ALU.add)
                        S.copy("scalar", Sbf[:, h0:h0 + 4, :], S32[:, h0:h0 + 4, :])
                        yield
                    if full:
                        pn = nb(pool)
                        for ci in range(4):
                            S.act(sqb[bt], ogT[:, h0 + ci, :], AF.Square)
                            S.mm(pn[0:16, :], E_bf[:, ci * 16:(ci + 1) * 16], sqb[bt], start=(ci == 0), stop=(ci == 3))
                            yield
                        rs_ = rs16b[bt]
                        S.act(rs_[0:16, :], pn[0:16, :], AF.Ln, bias=RMS_EPS, scale=1.0 / 128.0)
                        S.act(rs_[0:16, :], rs_[0:16, :], AF.Exp, scale=-0.5)
                        yield
                        while not sstate["Z_done"]:
                            yield
                        for ci in range(4):
                            h = h0 + ci
                            pb = nb(pool)
                            S.mm(pb[:, :], Ep[:, ci * 128:(ci + 1) * 128], rs_[0:16, :])
                            yield
                            t_ = tmpC if bt == 0 else tmpD
                            S.stt(t_, ogT[:, h, :], der[:, DR_NWH:DR_NWH + 1], pb[:, :], ALU.mult, ALU.mult)
                            S.tt(ogT[:, h, :], t_, zsb[h], ALU.mult)
                            yield

                def gen_z():
                    while not sstate["L_done"]:
                        yield
                    for h in range(8):
                        bt_ = h // 4
                        while sstate["zw"][bt_] is None:
                            yield
                        pz = inproj(sstate["zw"][bt_], h % 4, xTb, pool="L")
                        yield
                        S.act(tmpE, pz[:, :], AF.Tanh, scale=0.5)
                        S.stt(zsb[h], tmpE, 1.0, pz[:, :], ALU.add, ALU.mult)
                        yield
                    sstate["Z_done"] = True

                def gen_lru():
                    wa = wload(2, lru_wa.rearrange("h i j -> i h j"), HL, 128)
                    wi = wload(3, lru_wi.rearrange("h i j -> i h j"), HL, 128)
                    lxw = None
                    for hl in range(HL):
                        if hl in (0, 4, 8):
                            n_ = 512 if hl < 8 else 256
                            lxw = wload(4, w_in_v[:, :, C_LX + 128 * hl:C_LX + 128 * hl + n_], 8, n_)
                        pb = inproj(lxw, hl % 4, xTb, pool="L")
                        yield
                        pc_ = conv(pb, taill[:, hl, :], chp[:, CH_CWL + 4 * hl:CH_CWL + 4 * hl + 4], 2, "L")
                        yield
                        u = cv[2]
                        S.act(u, pc_[:, :], AF.Identity, bias=chp[:, CH_CBL + hl:CH_CBL + hl + 1])
                        S.copy("vector", ubf, u)
                        pr_ = nb("L")
                        S.mm(pr_[:, :], wa[:, hl, :], ubf)
                        pi_ = nb("L")
                        S.mm(pi_[:, :], wi[:, hl, :], ubf)
                        yield
                        th = tmpA[hl % 2]
                        S.act(th, pr_[:, :], AF.Tanh, bias=der[:, DR_HBA + hl:DR_HBA + hl + 1], scale=0.5)
                        thi = tmpB[hl % 2]
                        S.act(thi, pi_[:, :], AF.Tanh, bias=der[:, DR_HBI + hl:DR_HBI + hl + 1], scale=0.5)
                        a_ = tmpC
                        S.act(a_, th, AF.Exp, bias=der[:, DR_HLS + hl:DR_HLS + hl + 1],
                              scale=der[:, DR_HLS + hl:DR_HLS + hl + 1])
                        yield
                        ml = tmpD
                        S.act(ml, a_, AF.Square)
                        S.act(ml, ml, AF.Ln, bias=1.0, scale=-1.0)
                        S.act(ml, ml, AF.Exp, scale=0.5)
                        yield
                        first_pre = (not full) and blk == 0
                        if first_main or first_pre:
                            fcol = flag[:, 0:1] if first_main else der[:, DR_OMF:DR_OMF + 1]
                            t0_ = t0buf[:, 0:1]
                            S.ts(t0_, ml[:, 0:1], -1.0, ALU.mult, 1.0, ALU.add)
                            S.ts(t0_, t0_, fcol, ALU.mult)
                            S.tt(ml[:, 0:1], ml[:, 0:1], t0_, ALU.add)
                            if first_main:
                                S.ts(hst[:, hl:hl + 1], hst[:, hl:hl + 1], der[:, DR_OMF:DR_OMF + 1], ALU.mult)
                        bin_ = tmpE
                        S.stt(thi, thi, 1.0, u, ALU.add, ALU.mult)
                        S.stt(bin_, thi, 0.5, ml, ALU.mult, ALU.mult)
                        yield
                        hs = thi
                        S.op("vector", lambda e, hs=hs, a_=a_, bin_=bin_, hl=hl: e.tensor_tensor_scan(
                            out=hs, data0=a_, data1=bin_, initial=hst[:, hl:hl + 1], op0=ALU.mult, op1=ALU.add),
                            reads=[a_, bin_, hst[:, hl:hl + 1]], writes=[hs])
                        S.copy("vector", hst[:, hl:hl + 1], hs[:, TB - 1:TB])
                        if full:
                            S.copy("scalar", lroT[:, hl, :], hs)
                        yield
                    if full:
                        lgw = None
                        for hl in range(HL):
                            if hl in (0, 4, 8):
                                n_ = 512 if hl < 8 else 256
                                lgw = wload(4, w_in_v[:, :, C_LG + 128 * hl:C_LG + 128 * hl + n_], 8, n_)
                            pg = inproj(lgw, hl % 4, xTb, pool="L")
                            yield
                            x2 = tmpA[hl % 2]
                            S.act(x2, pg[:, :], AF.Square)
                            S.ts(x2, x2, GC1, ALU.mult, 1.0, ALU.add)
                            S.tt(x2, x2, pg[:, :], ALU.mult)
                            yield
                            S.act(x2, x2, AF.Tanh, scale=GC0)
                            gl_ = tmpB[hl % 2]
                            S.stt(gl_, x2, 1.0, pg[:, :], ALU.add, ALU.mult)
                            S.stt(lroT[:, hl, :], gl_, 0.5, lroT[:, hl, :], ALU.mult, ALU.mult)
                            yield

                def gen_lru_wrap():
                    yield from gen_lru()
                    sstate["L_done"] = True
                    if full:
                        ws["limit"] = blk * n_p2 + 3
                        w_issue_upto(blk * n_p2 + 3)

                gens = [[gen_delta(0), 0.0], [gen_delta(1), 0.0], [gen_lru_wrap(), 0.0]]
                if full:
                    gens.append([gen_z(), 0.0])
                S._track_stream = True
                while gens:
                    gens.sort(key=lambda ge: ge[1])
                    ge = gens[0]
                    S.last_fin = 0.0
                    try:
                        next(ge[0])
                        if S.last_fin > 0.0:
                            ge[1] = S.last_fin
                        else:
                            ge[1] += 0.5
                    except StopIteration:
                        gens.remove(ge)
                S._track_stream = False

                chk(5)
                if not full:
                    continue

                ws["limit"] = (blk + 1) * n_p2
                for hf in range(2):
                    gdw = wnext("gd%d" % hf)
                    glw = wnext("gl%d" % hf, keep=1)
                    bdw = wnext("bdn%d" % hf, keep=2)
                    blw = wnext("blru%d" % hf, keep=3)
                    for ci in range(4):
                        dc = hf * 4 + ci
                        pgd = inproj(gdw, ci, xTb)
                        pgl = inproj(glw, ci, xTb)
                        pyd = nb()
                        for kc in range(8):
                            S.mm(pyd[:, :], bdw[:, kc, ci * 128:(ci + 1) * 128], ogT[:, kc, :], start=(kc == 0), stop=(kc == 7))
                        pyl = nb()
                        for kc in range(HL):
                            S.mm(pyl[:, :], blw[:, kc, ci * 128:(ci + 1) * 128], lroT[:, kc, :], start=(kc == 0), stop=(kc == HL - 1))
                        S.act(gt1, pgd[:, :], AF.Tanh, bias=der[:, DR_HBM + dc:DR_HBM + dc + 1], scale=0.5)
                        S.act(gt2, pgl[:, :], AF.Tanh, bias=der[:, DR_HBM + 8 + dc:DR_HBM + 8 + dc + 1], scale=0.5)
                        S.stt(m1, gt1, 1.0, pyd[:, :], ALU.add, ALU.mult)
                        S.stt(m2, gt2, 1.0, pyl[:, :], ALU.add, ALU.mult)
                        S.tt(mrgT[:, dc, :], m1, m2, ALU.add)
                def ln_load(which):
                    S.dma(lnp.rearrange("p a d -> p (a d)"), lnp_d[:, which * 2 * D:(which + 1) * 2 * D])

                def ln_st(st):
                    S.act(rstd[:, st:st + 1], mv[:, st, 1:2], AF.Ln, bias=LN_EPS)
                    S.act(rstd[:, st:st + 1], rstd[:, st:st + 1], AF.Exp, scale=-0.5)
                    S.ts(h1[:, st, :], h1[:, st, :], mv[:, st, 0:1], ALU.subtract, rstd[:, st:st + 1], ALU.mult)
                    S.tt(h1[:, st, :], h1[:, st, :], lnp[:, 0, :], ALU.mult)
                    S.tt(h1[:, st, :], h1[:, st, :], lnp[:, 1, :], ALU.add)

                ln_load(0)
                wo = [wnext("wo0"), wnext("wo1", keep=1)]
                for st in range(4):
                    tsl = slice(st * 128, (st + 1) * 128)
                    S.dma(xtok, x_main[blk * TB + st * 128:blk * TB + (st + 1) * 128, :])
                    for hf in range(2):
                        pm = nb()
                        for kc in range(8):
                            S.mm(pm[:, :], mrgT[:, kc, tsl], wo[hf][:, kc, :], start=(kc == 0), stop=(kc == 7))
                        csl = slice(hf * 512, (hf + 1) * 512)
                        S.act(xtok[:, csl], xtok[:, csl], AF.Copy, scale=ALPHA)
                        S.stt(h1[:, st, csl], pm[:, :], 0.5, xtok[:, csl], ALU.mult, ALU.add)
                        S.op("vector", lambda e, st=st, hf=hf, csl=csl: e.bn_stats(out=stats[:, st, hf, :], in_=h1[:, st, csl]),
                             reads=[h1[:, st, csl]], writes=[stats[:, st, hf, :]])
                    S.op("vector", lambda e, st=st: e.bn_aggr(out=mv[:, st, :], in_=stats[:, st, :, :].rearrange("p a b -> p (a b)")),
                         reads=[stats[:, st, :, :]], writes=[mv[:, st, :]])
                    ln_st(st)
                    S.copy("scalar", h1b, h1[:, st, :])
                    pt = nb()
                    ptb = pt[:].bitcast(BF16)
                    for kc in range(8):
                        S.transpose(ptb[:, kc * 128:(kc + 1) * 128], h1b[:, kc * 128:(kc + 1) * 128], ident_bf)
                    S.copy("vector", h1T[:, :, st * 128:(st + 1) * 128], ptb[:, :].rearrange("p (k t) -> p k t", k=8))

                for i in range(6):
                    wg = wnext("wg%d" % i)
                    wu = wnext("wu%d" % i, keep=1)
                    for ci in range(4 if i < 5 else 2):
                        f = i * 4 + ci
                        pg = nb()
                        for kc in range(8):
                            S.mm(pg[:, :], wg[:, kc, ci * 128:(ci + 1) * 128], h1T[:, kc, :], start=(kc == 0), stop=(kc == 7))
                        pu = nb()
                        for kc in range(8):
                            S.mm(pu[:, :], wu[:, kc, ci * 128:(ci + 1) * 128], h1T[:, kc, :], start=(kc == 0), stop=(kc == 7))
                        sg = gt1 if f % 2 == 0 else gt2
                        S.act(sg, pg[:, :], AF.Silu)
                        S.tt(actT[:, f, :], pu[:, :], sg, ALU.mult)
                ln_load(1)
                for hf in range(2):
                    csl = slice(hf * 512, (hf + 1) * 512)
                    pfs = [nb() for _ in range(4)]
                    for j, (f0, nf) in enumerate(((0, 8), (8, 8), (16, 6))):
                        wd = wnext("wd%d_%d" % (hf, j))
                        for st in range(4):
                            for ff in range(nf):
                                f = f0 + ff
                                S.mm(pfs[st][:, :], actT[:, f, st * 128:(st + 1) * 128], wd[:, ff, :],
                                     start=(f == 0), stop=(f == NFC - 1))
                    for st in range(4):
                        S.stt(h1[:, st, csl], h1[:, st, csl], ALPHA, pfs[st][:, :], ALU.mult, ALU.add)
                        S.op("vector", lambda e, st=st, hf=hf, csl=csl: e.bn_stats(out=stats[:, st, hf, :], in_=h1[:, st, csl]),
                             reads=[h1[:, st, csl]], writes=[stats[:, st, hf, :]])
                for st in range(4):
                    S.op("vector", lambda e, st=st: e.bn_aggr(out=mv[:, st, :], in_=stats[:, st, :, :].rearrange("p a b -> p (a b)")),
                         reads=[stats[:, st, :, :]], writes=[mv[:, st, :]])
                    ln_st(st)
                    S.dma(out_d[blk * TB + st * 128:blk * TB + (st + 1) * 128, :], h1[:, st, :])


          except _Stop:
            break
        assert stage != 99 or ws["used"] == len(allg), (ws["used"], len(allg))
        with nc.Block() as block:
            S.replay(block)
    S.dbg_names = dbg_names
    return nc, S


def _consts():
    c = np.zeros((128, NCST), np.float32)
    i = np.arange(128)
    c[:, CS_ID:CS_ID + 128] = np.eye(128, dtype=np.float32)
    c[:, CS_U:CS_U + 128] = (i[:, None] <= i[None, :]).astype(np.float32)
    c[:, CS_ONE:CS_ONE + 128] = 1.0
    c[:, CS_POSL:CS_POSL + 128] = np.where(i[None, :] >= i[:, None], BIG, 0.0)
    c[:, CS_POSU:CS_POSU + 128] = np.where(i[None, :] < i[:, None], BIG, 0.0)
    for r in range(16):
        c[:, CS_E + r * 16 + r] = 1.0
        c[r, CS_EP + r * 128:CS_EP + (r + 1) * 128] = 1.0
    return c


def _prep_shared(inp):
    sh = {}
    sh["w_in"] = np.ascontiguousarray(inp["w_in"][0])
    sh["lru_wa"] = np.ascontiguousarray(inp["lru_w_a"][0])
    sh["lru_wi"] = np.ascontiguousarray(inp["lru_w_i"][0])
    sh["w_bdn"] = np.ascontiguousarray(inp["w_branch_dn"][0])
    sh["w_blru"] = np.ascontiguousarray(inp["w_branch_lru"][0])
    sh["w_out"] = np.ascontiguousarray(inp["w_out"][0])
    sh["w_g"] = np.ascontiguousarray(inp["w_ffn_gate"][0])
    sh["w_u"] = np.ascontiguousarray(inp["w_ffn_up"][0])
    sh["w_d"] = np.ascontiguousarray(inp["w_ffn_down"][0])
    sh["cst_in"] = _consts()
    chp = np.zeros((128, NCHP), np.float32)
    chp[:, CH_CWQ:CH_CWQ + 96] = inp["dn_conv_w"][0].reshape(4, 24, 128).transpose(2, 1, 0).reshape(128, 96)
    chp[:, CH_CWL:CH_CWL + 40] = inp["lru_conv_w"][0].reshape(4, 10, 128).transpose(2, 1, 0).reshape(128, 40)
    chp[:, CH_CBL:CH_CBL + 10] = inp["lru_conv_b"][0].reshape(10, 128).T
    chp[:, CH_BA:CH_BA + 10] = inp["lru_b_a"][0].reshape(10, 128).T
    chp[:, CH_BI:CH_BI + 10] = inp["lru_b_i"][0].reshape(10, 128).T
    chp[:, CH_LAM:CH_LAM + 10] = inp["lru_lambda"][0].reshape(10, 128).T
    chp[:, CH_BM:CH_BM + 16] = inp["b_merge_gate"][0].reshape(16, 128).T
    chp[:, CH_NW] = inp["dn_norm_w"][0]
    sh["chp_in"] = chp
    rowp = np.zeros((128, 16), np.float32)
    rowp[:, 0:8] = inp["dn_A_log"][0][None, :]
    rowp[:, 8:16] = inp["dn_dt_bias"][0][None, :]
    sh["rowp_in"] = rowp
    lnp = np.zeros((128, 4 * D), np.float32)
    for j, k in enumerate(("ln1_g", "ln1_b", "ln2_g", "ln2_b")):
        lnp[:, j * D:(j + 1) * D] = inp[k][0][None, :]
    sh["lnp_in"] = lnp
    return sh


def kernel(**inp):
    x = np.asarray(inp["x"], np.float32)
    B, T, _ = x.shape
    half = T // 2
    nblk = half // TB
    nc, _ = build(nblk, nblk)
    sh = _prep_shared({k: np.asarray(v, np.float32) for k, v in inp.items()})
    in_maps = []
    for b in range(B):
        for hs in range(2):
            m = dict(sh)
            xm = x[b, hs * half:(hs + 1) * half]
            m["x_main"] = np.ascontiguousarray(xm)
            m["xT_main"] = np.ascontiguousarray(xm.T)
            if hs == 0:
                m["xT_pre"] = np.zeros((D, half), np.float32)
                m["flag_in"] = np.ones((128, 1), np.float32)
            else:
                m["xT_pre"] = np.ascontiguousarray(x[b, 0:half].T)
                m["flag_in"] = np.zeros((128, 1), np.float32)
            in_maps.append(m)
    res = run_bass_kernel_spmd(nc, in_maps, core_ids=list(range(2 * B)))
    out = np.empty((B, T, D), np.float32)
    for b in range(B):
        for hs in range(2):
            out[b, hs * half:(hs + 1) * half] = res.results[b * 2 + hs]["out"]
    return out
```
